# Optimizing a Trainium2 kernel written in Bass

```python
import jax
import jax.numpy as jnp
from jax import lax
import numpy as np

D_MODEL = 1024
BATCH = 8
SEQ = 4096
DEPTH = 4
DEC_BATCH = 32
DEC_SEQ = 64
PAST_LEN = 2048

CHUNK = 64
N_MIXERS = 4
N_CONF = (DEPTH + 3) // 4
N_POOL = (DEPTH + 2) // 4
N_SCONV = (DEPTH + 1) // 4
N_HGRN = DEPTH // 4
CONF_WIDTH = 31
POOL_WINDOWS = (2, 4, 8, 16)
POOL_GROUPS = 4
POOL_GROUP_DIM = D_MODEL // POOL_GROUPS
POOL_HIST = max(POOL_WINDOWS) - 1
SCONV_WIDTH = 3
HGRN_HEAD_DIM = 128
HGRN_HEADS = D_MODEL // HGRN_HEAD_DIM
D_FF = 2816
FFN_WIDTH = 3
EPS = 1e-6

kernel_name = "hybrid_streaming_encoder_step"


def rmsnorm(x, g):
    xf = x.astype(jnp.float32)
    y = xf * lax.rsqrt(jnp.mean(xf * xf, axis=-1, keepdims=True) + EPS)
    return (y * g.astype(jnp.float32)).astype(x.dtype)


def causal_dwconv(x, hist, w):
    width = w.shape[0]
    xp = jnp.concatenate([hist.astype(x.dtype), x], axis=1)
    y = lax.conv_general_dilated(xp, w[:, None, :].astype(x.dtype), window_strides=(1,),
                                 padding='VALID', dimension_numbers=('NWC', 'WIO', 'NWC'),
                                 feature_group_count=x.shape[-1])
    return y, xp[:, xp.shape[1] - (width - 1):]


def conformer_conv(xn, hist, w1, b1, wdw, bdw, ln_g, ln_b, w2, b2):
    a, gate = jnp.split(xn @ w1 + b1, 2, axis=-1)
    u = a * jax.nn.sigmoid(gate)
    c, new_hist = causal_dwconv(u, hist, wdw)
    cf = (c + bdw).astype(jnp.float32)
    mu = jnp.mean(cf, axis=-1, keepdims=True)
    var = jnp.mean(jnp.square(cf - mu), axis=-1, keepdims=True)
    cn = ((cf - mu) * lax.rsqrt(var + EPS) * ln_g + ln_b).astype(xn.dtype)
    return jax.nn.silu(cn) @ w2 + b2, new_hist


def pool_mixer(xn, hist, pos0, w_pool, scale):
    b, t, d = xn.shape
    xp = jnp.concatenate([hist.astype(xn.dtype), xn], axis=1).astype(jnp.float32)
    c = jnp.concatenate([jnp.zeros((b, 1, d), jnp.float32), jnp.cumsum(xp, axis=1)], axis=1)
    end = c[:, POOL_HIST + 1:]
    pos = pos0 + jnp.arange(t)
    outs = []
    for g, w in enumerate(POOL_WINDOWS):
        lo, hi = g * POOL_GROUP_DIM, (g + 1) * POOL_GROUP_DIM
        start = c[:, POOL_HIST + 1 - w:POOL_HIST + 1 - w + t, lo:hi]
        cnt = jnp.minimum(pos + 1, w).astype(jnp.float32)[None, :, None]
        diff = (end[..., lo:hi] - start) / cnt - xp[:, POOL_HIST:, lo:hi]
        outs.append(diff.astype(xn.dtype) @ w_pool[g])
    y = jnp.concatenate(outs, axis=-1) * scale
    return y, xp[:, xp.shape[1] - POOL_HIST:].astype(xn.dtype)


def short_conv_mixer(xn, hist, w_in, w_dw, w_out):
    bg, cg, v = jnp.split(xn @ w_in, 3, axis=-1)
    c, new_hist = causal_dwconv(cg * v, hist, w_dw)
    return (bg * c) @ w_out, new_hist


def hgrn2_chunk(s, inp):
    q, k, v, logf = inp
    L = q.shape[1]
    bcum = jnp.cumsum(logf, axis=1)
    qd = q * jnp.exp(bcum)
    kd = k * jnp.exp(-bcum)
    scores = jnp.einsum('bthk,bshk->bhts', qd, kd)
    mask = jnp.tril(jnp.ones((L, L), dtype=bool))
    scores = jnp.where(mask, scores, 0.0)
    o = jnp.einsum('bhts,bshv->bthv', scores, v) + jnp.einsum('bthk,bhkv->bthv', qd, s)
    b_last = bcum[:, -1]
    s_new = s * jnp.exp(b_last)[..., None] + jnp.einsum(
        'bshk,bshv->bhkv', k * jnp.exp(b_last[:, None] - bcum), v)
    return s_new, o


def hgrn2_mixer(xn, s0, lb, w_q, w_f, w_i, w_g, w_o, norm_g):
    b, t, d = xn.shape
    shp = (b, t, HGRN_HEADS, HGRN_HEAD_DIM)
    q = jax.nn.silu((xn @ w_q).astype(jnp.float32)).reshape(shp)
    lbh = lb.astype(jnp.float32).reshape(HGRN_HEADS, HGRN_HEAD_DIM)
    f = lbh + (1.0 - lbh) * jax.nn.sigmoid((xn @ w_f).astype(jnp.float32)).reshape(shp)
    k = 1.0 - f
    logf = jnp.log(f)
    v = (xn @ w_i).astype(jnp.float32).reshape(shp)
    s0 = s0.astype(jnp.float32)
    if t <= CHUNK:
        s_new, o = hgrn2_chunk(s0, (q, k, v, logf))
    else:
        n = t // CHUNK
        to_blocks = lambda a: jnp.moveaxis(a.reshape(b, n, CHUNK, HGRN_HEADS, HGRN_HEAD_DIM), 1, 0)
        s_new, o = lax.scan(hgrn2_chunk, s0, (to_blocks(q), to_blocks(k), to_blocks(v), to_blocks(logf)))
        o = jnp.moveaxis(o, 0, 1).reshape(shp)
    on = o * lax.rsqrt(jnp.mean(o * o, axis=-1, keepdims=True) + EPS) * norm_g.astype(jnp.float32)
    on = on.reshape(b, t, d).astype(xn.dtype) * jax.nn.silu(xn @ w_g)
    return on @ w_o, s_new


def conv_ffn(xn, hist, w_gate, w_up, w_dw, b_dw, w_down):
    g, new_hist = causal_dwconv(xn @ w_gate, hist, w_dw)
    h = jax.nn.silu(g + b_dw) * (xn @ w_up)
    return h @ w_down, new_hist


def trunk(x, pos0, conf_hist, pool_hist, sconv_hist, hgrn_state, ffn_hist, p):
    lb_all = jnp.cumsum(jax.nn.softmax(p['hgrn_lb_logits'].astype(jnp.float32), axis=0), axis=0)
    lb_all = lb_all - lb_all[0:1]
    new_conf, new_pool, new_sconv, new_hgrn, new_ffn = [], [], [], [], []
    for i in range(DEPTH):
        m, j = i % N_MIXERS, i // N_MIXERS
        xn = rmsnorm(x, p['norm_mix'][i])
        if m == 0:
            y, h = conformer_conv(xn, conf_hist[j], p['conf_w_pw1'][j], p['conf_b_pw1'][j],
                                  p['conf_w_dw'][j], p['conf_b_dw'][j], p['conf_ln_g'][j],
                                  p['conf_ln_b'][j], p['conf_w_pw2'][j], p['conf_b_pw2'][j])
            new_conf.append(h)
        elif m == 1:
            y, h = pool_mixer(xn, pool_hist[j], pos0, p['pool_w'][j], p['pool_scale'][j])
            new_pool.append(h)
        elif m == 2:
            y, h = short_conv_mixer(xn, sconv_hist[j], p['sconv_w_in'][j], p['sconv_w_dw'][j],
                                    p['sconv_w_out'][j])
            new_sconv.append(h)
        else:
            y, h = hgrn2_mixer(xn, hgrn_state[j], lb_all[i], p['hgrn_w_q'][j], p['hgrn_w_f'][j],
                               p['hgrn_w_i'][j], p['hgrn_w_g'][j], p['hgrn_w_o'][j],
                               p['hgrn_norm_g'][j])
            new_hgrn.append(h.astype(hgrn_state.dtype))
        x = x + y
        xn = rmsnorm(x, p['norm_ffn'][i])
        y, h = conv_ffn(xn, ffn_hist[i], p['ffn_w_gate'][i], p['ffn_w_up'][i], p['ffn_w_dw'][i],
                        p['ffn_b_dw'][i], p['ffn_w_down'][i])
        new_ffn.append(h)
        x = x + y
    return (rmsnorm(x, p['norm_final']), jnp.stack(new_conf), jnp.stack(new_pool),
            jnp.stack(new_sconv), jnp.stack(new_hgrn), jnp.stack(new_ffn))


def setup_inputs(seed: int = 0) -> dict:
    key = jax.random.key(seed)
    keys = list(jax.random.split(key, 40))

    def nrm(idx, shape, s):
        return jax.random.normal(keys[idx], shape, jnp.float32) * s

    D = D_MODEL
    return {
        'x_prompt': nrm(0, (BATCH, SEQ, D), 1.0),
        'x_sample': nrm(1, (DEC_BATCH, DEC_SEQ, D), 1.0),
        'state_conformer_conv': nrm(2, (N_CONF, DEC_BATCH, CONF_WIDTH - 1, D), 0.5),
        'state_pool': nrm(3, (N_POOL, DEC_BATCH, POOL_HIST, D), 1.0),
        'state_short_conv': nrm(4, (N_SCONV, DEC_BATCH, SCONV_WIDTH - 1, D), 0.5),
        'state_hgrn': nrm(5, (N_HGRN, DEC_BATCH, HGRN_HEADS, HGRN_HEAD_DIM, HGRN_HEAD_DIM), 0.5),
        'state_ffn_conv': nrm(6, (DEPTH, DEC_BATCH, FFN_WIDTH - 1, D_FF), 1.0),
        'norm_mix': 1.0 + nrm(7, (DEPTH, D), 0.05),
        'norm_ffn': 1.0 + nrm(8, (DEPTH, D), 0.05),
        'norm_final': 1.0 + nrm(9, (D,), 0.05),
        'conf_w_pw1': nrm(10, (N_CONF, D, 2 * D), D ** -0.5),
        'conf_b_pw1': nrm(11, (N_CONF, 2 * D), 0.02),
        'conf_w_dw': nrm(12, (N_CONF, CONF_WIDTH, D), CONF_WIDTH ** -0.5),
        'conf_b_dw': nrm(13, (N_CONF, D), 0.02),
        'conf_ln_g': 1.0 + nrm(14, (N_CONF, D), 0.05),
        'conf_ln_b': nrm(15, (N_CONF, D), 0.02),
        'conf_w_pw2': nrm(16, (N_CONF, D, D), D ** -0.5),
        'conf_b_pw2': nrm(17, (N_CONF, D), 0.02),
        'pool_w': nrm(18, (N_POOL, POOL_GROUPS, POOL_GROUP_DIM, POOL_GROUP_DIM), POOL_GROUP_DIM ** -0.5),
        'pool_scale': 0.5 + nrm(19, (N_POOL, D), 0.05),
        'sconv_w_in': nrm(20, (N_SCONV, D, 3 * D), D ** -0.5),
        'sconv_w_dw': nrm(21, (N_SCONV, SCONV_WIDTH, D), SCONV_WIDTH ** -0.5),
        'sconv_w_out': nrm(22, (N_SCONV, D, D), D ** -0.5),
        'hgrn_w_q': nrm(23, (N_HGRN, D, D), D ** -0.5),
        'hgrn_w_f': nrm(24, (N_HGRN, D, D), D ** -0.5),
        'hgrn_w_i': nrm(25, (N_HGRN, D, D), D ** -0.5),
        'hgrn_w_g': nrm(26, (N_HGRN, D, D), D ** -0.5),
        'hgrn_w_o': nrm(27, (N_HGRN, D, D), D ** -0.5),
        'hgrn_lb_logits': nrm(28, (DEPTH, D), 0.1),
        'hgrn_norm_g': 1.0 + nrm(29, (N_HGRN, HGRN_HEAD_DIM), 0.05),
        'ffn_w_gate': nrm(30, (DEPTH, D, D_FF), D ** -0.5),
        'ffn_w_up': nrm(31, (DEPTH, D, D_FF), D ** -0.5),
        'ffn_w_dw': nrm(32, (DEPTH, FFN_WIDTH, D_FF), FFN_WIDTH ** -0.5),
        'ffn_b_dw': nrm(33, (DEPTH, D_FF), 0.02),
        'ffn_w_down': nrm(34, (DEPTH, D_FF, D), D_FF ** -0.5),
    }


def reference(x_prompt, x_sample, state_conformer_conv, state_pool, state_short_conv, state_hgrn,
              state_ffn_conv, norm_mix, norm_ffn, norm_final, conf_w_pw1, conf_b_pw1, conf_w_dw,
              conf_b_dw, conf_ln_g, conf_ln_b, conf_w_pw2, conf_b_pw2, pool_w, pool_scale,
              sconv_w_in, sconv_w_dw, sconv_w_out, hgrn_w_q, hgrn_w_f, hgrn_w_i, hgrn_w_g, hgrn_w_o,
              hgrn_lb_logits, hgrn_norm_g, ffn_w_gate, ffn_w_up, ffn_w_dw, ffn_b_dw, ffn_w_down):
    p = {
        'norm_mix': norm_mix, 'norm_ffn': norm_ffn, 'norm_final': norm_final,
        'conf_w_pw1': conf_w_pw1, 'conf_b_pw1': conf_b_pw1, 'conf_w_dw': conf_w_dw,
        'conf_b_dw': conf_b_dw, 'conf_ln_g': conf_ln_g, 'conf_ln_b': conf_ln_b,
        'conf_w_pw2': conf_w_pw2, 'conf_b_pw2': conf_b_pw2,
        'pool_w': pool_w, 'pool_scale': pool_scale,
        'sconv_w_in': sconv_w_in, 'sconv_w_dw': sconv_w_dw, 'sconv_w_out': sconv_w_out,
        'hgrn_w_q': hgrn_w_q, 'hgrn_w_f': hgrn_w_f, 'hgrn_w_i': hgrn_w_i, 'hgrn_w_g': hgrn_w_g,
        'hgrn_w_o': hgrn_w_o, 'hgrn_lb_logits': hgrn_lb_logits, 'hgrn_norm_g': hgrn_norm_g,
        'ffn_w_gate': ffn_w_gate, 'ffn_w_up': ffn_w_up, 'ffn_w_dw': ffn_w_dw,
        'ffn_b_dw': ffn_b_dw, 'ffn_w_down': ffn_w_down,
    }
    b, dt = x_prompt.shape[0], x_prompt.dtype
    zc = jnp.zeros((N_CONF, b, CONF_WIDTH - 1, D_MODEL), dt)
    zp = jnp.zeros((N_POOL, b, POOL_HIST, D_MODEL), dt)
    zs = jnp.zeros((N_SCONV, b, SCONV_WIDTH - 1, D_MODEL), dt)
    zh = jnp.zeros((N_HGRN, b, HGRN_HEADS, HGRN_HEAD_DIM, HGRN_HEAD_DIM), dt)
    zf = jnp.zeros((DEPTH, b, FFN_WIDTH - 1, D_FF), dt)
    y_prompt, conf_p, pool_p, sconv_p, hgrn_p, ffn_p = trunk(x_prompt, 0, zc, zp, zs, zh, zf, p)
    y_sample, conf_s, pool_s, sconv_s, hgrn_s, ffn_s = trunk(
        x_sample, PAST_LEN, state_conformer_conv, state_pool, state_short_conv, state_hgrn,
        state_ffn_conv, p)
    return (y_prompt, y_sample, conf_p, conf_s, pool_p, pool_s, sconv_p, sconv_s,
            hgrn_p, hgrn_s, ffn_p, ffn_s)
```

```python
import numpy as np
from contextlib import ExitStack
import concourse.bass as bass
import concourse.mybir as mybir
from concourse.bass_utils import run_bass_kernel_spmd

F32 = mybir.dt.float32
BF16 = mybir.dt.bfloat16
AF = mybir.ActivationFunctionType
ALU = mybir.AluOpType

D = 1024
DFF = 2816
NC8 = 8
NF = 22
EPS = 1e-6
PAST_LEN = 2048


def _isz(dt):
    return 2 if dt == BF16 else 4


class Res:
    __slots__ = ("name", "acc", "frozen")

    def __init__(self, name):
        self.name = name
        self.acc = []
        self.frozen = False


class Op:
    __slots__ = ("eng", "fn", "deps", "sig", "cnt", "dma", "dsem", "dval", "dgrp")

    def __init__(self, eng, fn, dma=False):
        self.eng = eng
        self.fn = fn
        self.deps = []
        self.sig = False
        self.cnt = 0
        self.dma = dma
        self.dsem = None
        self.dval = 0
        self.dgrp = None


class View:
    __slots__ = ("res", "ap", "lo", "hi")

    def __init__(self, res, ap, gran=1):
        self.res = res
        self.ap = ap
        a = ap.ap
        isz = _isz(ap.dtype)
        pstep = a[0][0]
        lo = ap.offset % pstep if pstep > 0 else ap.offset
        ext = 1
        for (st, cn) in a[1:]:
            ext += (cn - 1) * abs(st)
        lo *= isz
        hi = lo + ext * isz
        if gran > 1:
            lo = (lo // gran) * gran
            hi = -((-hi) // gran) * gran
        self.lo = lo
        self.hi = hi

    @property
    def iv(self):
        return (self.res, self.lo, self.hi)


class TT:
    def __init__(self, name, ap_or_handle, gran=1, res=None):
        self.res = res if res is not None else Res(name)
        self.h = ap_or_handle
        self.gran = gran

    def __getitem__(self, idx):
        return View(self.res, self.h[idx], self.gran)

    def sub(self, ap):
        return TT(self.res.name, ap, self.gran, self.res)

    def freeze(self):
        self.res.frozen = True


class Prog:
    ENGS = ("pe", "act", "dve", "pool", "sp")

    def __init__(self, nc):
        self.nc = nc
        self.ops = {e: [] for e in self.ENGS}
        self.dma_keys = {}

    def add(self, eng, fn, reads=(), writes=(), dma_key=None):
        op = Op(eng, fn, dma=dma_key is not None)
        if dma_key is not None:
            ent = self.dma_keys.setdefault(dma_key, [None, 0, None])
            ent[1] += 16
            op.dsem = dma_key
            op.dval = ent[1]
            if ent[2] is None:
                ent[2] = [0]
            ent[2][0] = ent[1]
            op.dgrp = ent[2]
        deps = op.deps
        for v in reads:
            res, lo, hi = v.res, v.lo, v.hi
            for a in res.acc:
                if a[3] and a[0] < hi and lo < a[1]:
                    self._dep(op, a[2], deps)
            if not res.frozen:
                if not op.dma:
                    for a in res.acc:
                        if (not a[3]) and a[0] == lo and a[1] == hi and a[2].eng == eng and not a[2].dma:
                            a[2] = op
                            break
                    else:
                        res.acc.append([lo, hi, op, False])
                else:
                    res.acc.append([lo, hi, op, False])
        for v in writes:
            res, lo, hi = v.res, v.lo, v.hi
            assert not res.frozen, res.name
            keep = []
            for a in res.acc:
                if a[0] < hi and lo < a[1]:
                    if a[2] is not op:
                        self._dep(op, a[2], deps)
                    if lo <= a[0] and a[1] <= hi:
                        continue
                keep.append(a)
            keep.append([lo, hi, op, True])
            res.acc = keep
        self.ops[eng].append(op)
        return op

    def _dep(self, op, prod, deps):
        if prod is op:
            return
        if (not prod.dma) and prod.eng == op.eng and op.eng == "pe" and not op.dma:
            return
        if prod not in deps:
            deps.append(prod)
            if not prod.dma:
                prod.sig = True
            else:
                ent = self.dma_keys[prod.dsem]
                if ent[2] is prod.dgrp:
                    ent[2] = None

    def emit(self, es, final_wait_eng="sp"):
        nc = self.nc
        esem = {e: es.enter_context(nc.semaphore("s_" + e)) for e in self.ENGS}
        for k, ent in self.dma_keys.items():
            ent[0] = es.enter_context(nc.semaphore("d_" + str(k)))
        for e in self.ENGS:
            c = 0
            for op in self.ops[e]:
                if op.sig and not op.dma:
                    c += 1
                    op.cnt = c
        block = es.enter_context(nc.Block())
        engobj = {"pe": "tensor", "act": "scalar", "dve": "vector", "pool": "gpsimd", "sp": "sync"}
        dma_keys = self.dma_keys

        def run(e, eng):
            known = {}
            for op in self.ops[e]:
                need = {}
                for p in op.deps:
                    if p.dma:
                        k = ("d", p.dsem)
                        s = dma_keys[p.dsem][0]
                        v = p.dgrp[0]
                    else:
                        k = ("e", p.eng)
                        s = esem[p.eng]
                        v = p.cnt
                    if known.get(k, 0) >= v:
                        continue
                    if k not in need or need[k][1] < v:
                        need[k] = (s, v)
                for k, (s, v) in need.items():
                    eng.wait_ge(s, v)
                    known[k] = v
                ins = op.fn(eng)
                if op.dma:
                    ins.then_inc(dma_keys[op.dsem][0], 16)
                elif op.sig:
                    ins.then_inc(esem[e], 1)
            if e == final_wait_eng:
                for k, ent in dma_keys.items():
                    if known.get(("d", k), 0) < ent[1]:
                        eng.wait_ge(ent[0], ent[1])

        for e in self.ENGS:
            if not self.ops[e] and e != final_wait_eng:
                continue
            deco = getattr(block, engobj[e])

            def mk(e=e):
                def body(eng):
                    run(e, eng)
                return body
            deco(mk())


def _vec_layout():
    off = {}
    c = 0
    for name, n in [("norm_mix", 32), ("norm_ffn", 32), ("norm_final", 8), ("conf_b1", 16),
                    ("conf_wdw", 248), ("conf_bdw", 8), ("conf_lng", 8), ("conf_lnb", 8),
                    ("conf_b2", 8), ("pool_scale", 8), ("sconv_wdw", 24), ("lb_logits", 32),
                    ("hgrn_ng", 1), ("ffn_wdw", 264), ("ffn_bdw", 88)]:
        off[name] = c
        c += n
    return off, c


VOFF, NVEC = _vec_layout()


def _fm(v):
    v = np.asarray(v, np.float32)
    return np.ascontiguousarray(v.reshape(-1, 128).T)


def build_vec(inp):
    cols = []
    cols.append(np.concatenate([_fm(inp["norm_mix"][i]) for i in range(4)], axis=1))
    cols.append(np.concatenate([_fm(inp["norm_ffn"][i]) for i in range(4)], axis=1))
    cols.append(_fm(inp["norm_final"]))
    cols.append(_fm(inp["conf_b_pw1"][0]))
    cols.append(np.concatenate([_fm(inp["conf_w_dw"][0][j]) for j in range(31)], axis=1))
    cols.append(_fm(inp["conf_b_dw"][0]))
    cols.append(_fm(inp["conf_ln_g"][0]))
    cols.append(_fm(inp["conf_ln_b"][0]))
    cols.append(_fm(inp["conf_b_pw2"][0]))
    cols.append(_fm(inp["pool_scale"][0]))
    cols.append(np.concatenate([_fm(inp["sconv_w_dw"][0][j]) for j in range(3)], axis=1))
    cols.append(np.concatenate([_fm(inp["hgrn_lb_logits"][i]) for i in range(4)], axis=1))
    cols.append(np.asarray(inp["hgrn_norm_g"][0], np.float32).reshape(128, 1))
    cols.append(np.concatenate([_fm(inp["ffn_w_dw"][i][j]) for i in range(4) for j in range(3)], axis=1))
    cols.append(np.concatenate([_fm(inp["ffn_b_dw"][i]) for i in range(4)], axis=1))
    v = np.concatenate(cols, axis=1)
    assert v.shape == (128, NVEC), v.shape
    return np.ascontiguousarray(v, np.float32)


def build_consts(TP, T):
    ident = np.eye(128, dtype=np.float32)
    s = np.arange(128)[:, None]
    t = np.arange(128)[None, :]
    mask = (t >= s).astype(np.float32)
    mask4 = mask
    rm = np.ones((512,), np.float32)
    rm[0:512:128] = 0.0
    rmask = np.tile(rm[None, :], (128, 1))
    invc = np.zeros((128, 4, 16), np.float32)
    for g, w in enumerate((2, 4, 8, 16)):
        for tt_ in range(16):
            invc[:, g, tt_] = 1.0 / min(tt_ + 1, w)
    return ident, mask4, rmask, invc.reshape(128, 64)


def build(n_pass, TP, NTMAX=512):
    TS = 64
    T = TP + TS
    NG = TP // 128
    nc = bass.Bass("TRN2", target_bir_lowering=False)
    es = ExitStack()

    def din(name, shape):
        return nc.dram_tensor(name, list(shape), F32, kind="ExternalInput").ap()

    def dout(name, shape):
        return nc.dram_tensor(name, list(shape), F32, kind="ExternalOutput").ap()

    xp_d = din("xp", [n_pass * TP, D])
    xs_d = din("xs", [n_pass, TS, D])
    sconf_d = din("s_conf", [n_pass, 30, D])
    spool_d = din("s_pool", [n_pass, 15, D])
    ssc_d = din("s_sconv", [n_pass, 2, D])
    shg_d = din("s_hgrn", [n_pass, 8, 128, 128])
    sffn_d = din("s_ffn", [4, n_pass, 2, DFF])
    vec_d = din("vec", [128, NVEC])
    ident_d = din("c_ident", [128, 128])
    mask_d = din("c_mask4", [128, 128])
    rmask_d = din("c_rmask", [128, 512])
    invc_d = din("c_invc", [128, 64])
    w1_d = din("conf_w_pw1", [D, 2 * D])
    w2_d = din("conf_w_pw2", [D, D])
    pw_d = din("pool_w", [4, 256, 256])
    swin_d = din("sconv_w_in", [D, 3 * D])
    swout_d = din("sconv_w_out", [D, D])
    hq_d = din("hgrn_w_q", [D, D])
    hf_d = din("hgrn_w_f", [D, D])
    hi_d = din("hgrn_w_i", [D, D])
    hg_d = din("hgrn_w_g", [D, D])
    ho_d = din("hgrn_w_o", [D, D])
    fg_d = din("ffn_w_gate", [4, D, DFF])
    fu_d = din("ffn_w_up", [4, D, DFF])
    fd_d = din("ffn_w_down", [4, DFF, D])

    yp_d = dout("yp", [n_pass * TP, D])
    ys_d = dout("ys", [n_pass, TS, D])
    oconf_p = dout("o_conf_p", [30, D])
    oconf_s = dout("o_conf_s", [n_pass, 30, D])
    opool_p = dout("o_pool_p", [15, D])
    opool_s = dout("o_pool_s", [n_pass, 15, D])
    osc_p = dout("o_sconv_p", [2, D])
    osc_s = dout("o_sconv_s", [n_pass, 2, D])
    ohg_p = dout("o_hgrn_p", [8, 128, 128])
    ohg_s = dout("o_hgrn_s", [n_pass, 8, 128, 128])
    offn_p = dout("o_ffn_p", [4, 2, DFF])
    offn_s = dout("o_ffn_s", [4, n_pass, 2, DFF])

    with es:
        P = Prog(nc)

        def sb(name, shape, dt):
            return TT(name, es.enter_context(nc.sbuf_tensor("sb_" + name, list(shape), dt)))

        x = sb("x", [128, NC8, T], F32)
        xn = sb("xn", [128, NC8, T], BF16)
        NSCR = 28800
        scr = sb("scr", [128, NSCR], BF16)
        NSLOT = 3
        SLOT = 5632
        wring = sb("wring", [128, NSLOT, SLOT], BF16)
        vec = sb("vec", [128, NVEC], F32)
        ident = sb("ident", [128, 128], F32)
        identb = sb("identb", [128, 128], BF16)
        onesb = sb("onesb", [128, 128], BF16)
        mask4 = sb("mask4", [128, 128], F32)
        rmask = sb("rmask", [128, 512], F32)
        invc = sb("invc", [128, 64], F32)
        stg = sb("stg", [128, 2, 1024], F32)
        rstd = sb("rstd", [128, T], F32)
        tmpa = sb("tmpa", [128, 2, 512], F32)
        tmpb = sb("tmpb", [128, 2, 512], F32)
        sqb = sb("sqb", [128, 6, 512], BF16)
        gbuf = sb("gbuf", [128, 2, 516], F32)
        accb = sb("accb", [128, 2, 512], F32)
        silb = sb("silb", [128, 2, 512], F32)
        confh_p = sb("confh_p", [128, NC8, 30], BF16)
        confh_s = sb("confh_s", [128, NC8, 30], BF16)
        utail = sb("utail", [128, 2, NC8, 32], F32)
        poolh_p = sb("poolh_p", [128, NC8, 15], F32)
        poolh_s = sb("poolh_s", [128, NC8, 15], F32)
        sch_p = sb("sch_p", [128, NC8, 2], F32)
        sch_s = sb("sch_s", [128, NC8, 2], F32)
        ffnh_p = sb("ffnh_p", [128, 4, NF, 2], F32)
        ffnh_s = sb("ffnh_s", [128, 4, NF, 2], F32)
        ffnt_s = sb("ffnt_s", [128, NF, 2], F32)
        S32 = sb("S32", [128, 2, 8, 128], F32)
        Sbf = sb("Sbf", [128, 2, 8, 128], BF16)
        lbv = sb("lbv", [128, 8], F32)
        omlv = sb("omlv", [128, 8], F32)
        nomlv = sb("nomlv", [128, 8], F32)
        homlv = sb("homlv", [128, 8], F32)
        lbhv = sb("lbhv", [128, 8], F32)
        lbt = sb("lbt", [128, 8, 4], F32)
        lbs = sb("lbs", [128, 8], F32)
        eblast = sb("eblast", [128, 2, 16], F32)
        Setmp = sb("Setmp", [128, 2, 128], F32)
        pst = TT("psum", es.enter_context(nc.psum_tensor("psum", [128, 8, 512], F32)), gran=2048)

        bank_ctr = [0]

        reserved = set()

        def bank():
            while True:
                b = bank_ctr[0] % 8
                bank_ctr[0] += 1
                if b not in reserved:
                    return b

        rot = {}

        def nxt(name, n=2):
            rot[name] = (rot.get(name, -1) + 1) % n
            return rot[name]

        def apof(v):
            return v.ap if isinstance(v, View) else v

        def rd(*vs):
            return [v for v in vs if isinstance(v, View)]

        def mm(out, lhsT, rhs, start, stop):
            P.add("pe", lambda e: e.matmul(out.ap, lhsT=lhsT.ap, rhs=rhs.ap, start=start, stop=stop),
                  reads=[lhsT, rhs], writes=[out])

        def tr(out, in_, idv):
            P.add("pe", lambda e: e.transpose(out.ap, in_.ap, idv.ap), reads=[in_, idv], writes=[out])

        def act(out, in_, func, bias=None, scale=None, eng="act"):
            kw = {}
            if bias is not None:
                kw["bias"] = apof(bias)
            if scale is not None:
                kw["scale"] = apof(scale)
            P.add(eng, lambda e: e.activation(out=out.ap, in_=in_.ap, func=func, **kw),
                  reads=[in_] + rd(bias, scale), writes=[out])

        def tt(out, a, b, op, eng="dve"):
            P.add(eng, lambda e: e.tensor_tensor(out=out.ap, in0=a.ap, in1=b.ap, op=op),
                  reads=[a, b], writes=[out])

        def ts(out, a, s1, s2, op0, op1=None, eng="dve"):
            if op1 is None:
                P.add(eng, lambda e: e.tensor_scalar(out=out.ap, in0=a.ap, scalar1=apof(s1), scalar2=None, op0=op0),
                      reads=[a] + rd(s1), writes=[out])
            else:
                P.add(eng, lambda e: e.tensor_scalar(out=out.ap, in0=a.ap, scalar1=apof(s1), scalar2=apof(s2),
                                                     op0=op0, op1=op1),
                      reads=[a] + rd(s1, s2), writes=[out])

        def stt(out, a, s, b, op0, op1):
            P.add("dve", lambda e: e.scalar_tensor_tensor(out=out.ap, in0=a.ap, scalar=apof(s), in1=b.ap,
                                                          op0=op0, op1=op1),
                  reads=[a, b] + rd(s), writes=[out])

        def cp(out, in_, eng="act"):
            if eng == "act":
                P.add("act", lambda e: e.activation(out=out.ap, in_=in_.ap, func=AF.Copy), reads=[in_], writes=[out])
            else:
                P.add(eng, lambda e: e.tensor_copy(out=out.ap, in_=in_.ap), reads=[in_], writes=[out])

        def memset(v, val, eng="dve"):
            P.add(eng, lambda e: e.memset(v.ap, val), writes=[v])

        def dma_in(dst, src_ap, key, eng="sp"):
            P.add(eng, lambda e: e.dma_start(out=dst.ap, in_=src_ap), writes=[dst], dma_key=key)

        def dma_out(dst_ap, src, key, eng="sp"):
            P.add(eng, lambda e: e.dma_start(out=dst_ap, in_=src.ap), reads=[src], dma_key=key)

        def V(name, c0=0, n=1):
            o = VOFF[name] + c0
            return vec[:, o:o + n]

        blocks = []

        def wblock(parts):
            blocks.append(parts)
            return len(blocks) - 1

        wstate = {"issued": 0}

        def issue_block(i):
            s = i % NSLOT
            off = 0
            for (w2d, col0, ncols, KC) in blocks[i]:
                src = w2d.rearrange("(kc p) n -> p kc n", p=128)[:, :, col0:col0 + ncols]
                dstv = View(wring.res, wring.h[:, s, off:off + KC * ncols].rearrange("p (k n) -> p k n", k=KC))
                P.add("pool", lambda e, dstv=dstv, src=src: e.dma_start(out=dstv.ap, in_=src),
                      writes=[dstv], dma_key="w%d" % s)
                off += KC * ncols
            assert off <= SLOT

        class WB:
            def __init__(self, i):
                self.i = i
                self.s = i % NSLOT
                self.offs = []
                off = 0
                for (w2d, col0, ncols, KC) in blocks[i]:
                    self.offs.append((off, ncols, KC))
                    off += KC * ncols

            def lhsT(self, part, kc, c0, n=128):
                off, ncols, KC = self.offs[part]
                b = off + kc * ncols + c0
                return wring[:, self.s, b:b + n]

        def wget(i):
            assert i >= wstate.get("last", 0), (i, wstate)
            wstate["last"] = i
            while wstate["issued"] < min(len(blocks), i + NSLOT):
                issue_block(wstate["issued"])
                wstate["issued"] += 1
            return WB(i)

        plan = []
        for p in range(n_pass):
            d = {}
            d["w1"] = [wblock([(w1_d, 256 * j, 256, 8), (w1_d, 1024 + 256 * j, 256, 8)]) for j in range(4)]
            d["w2"] = [wblock([(w2_d, 512 * j, 512, 8)]) for j in range(2)]
            d["ffn"] = []
            for L in range(4):
                if L == 1:
                    d["pool"] = wblock([(pw_d[g], 0, 256, 2) for g in range(4)])
                if L == 2:
                    d["swin"] = [wblock([(swin_d, 128 * j, 128, 8), (swin_d, 1024 + 128 * j, 128, 8),
                                         (swin_d, 2048 + 128 * j, 128, 8)]) for j in range(8)]
                    d["swout"] = [wblock([(swout_d, 512 * j, 512, 8)]) for j in range(2)]
                if L == 3:
                    d["hqf"] = []
                    d["hg"] = []
                    for j in range(4):
                        d["hqf"].append(wblock([(hq_d, 256 * j, 256, 8), (hf_d, 256 * j, 256, 8)]))
                        d["hg"].append(wblock([(hi_d, 256 * j, 256, 8), (hg_d, 256 * j, 256, 8)]))
                    d["ho"] = [wblock([(ho_d, 512 * j, 512, 8)]) for j in range(2)]
                gu = [wblock([(fg_d[L], 256 * j, 256, 8), (fu_d[L], 256 * j, 256, 8)]) for j in range(11)]
                dn = [wblock([(fd_d[L], 256 * j, 256, NF)]) for j in range(4)]
                d["ffn"].append((gu, dn))
            plan.append(d)

        ntiles = []
        n0 = 0
        while n0 < TP:
            n1 = min(TP, n0 + NTMAX)
            ntiles.append((n0, n1))
            n0 = n1
        ntiles.append((TP, T))
        groups = [(g * 128, 128, 0) for g in range(NG)] + [(TP, TS, 1)]

        def ecol(n, H):
            return n + H if n < TP else n + 2 * H

        dma_in(vec[:, :], vec_d, "c0")
        dma_in(ident[:, :], ident_d, "c0")
        dma_in(mask4[:, :], mask_d, "c0")
        dma_in(rmask[:, :], rmask_d, "c0")
        dma_in(invc[:, :], invc_d, "c0")
        cp(identb[:, :], ident[:, :])
        memset(onesb[:, :], 1.0)
        memset(confh_p[:, :, :], 0.0)
        memset(poolh_p[:, :, :], 0.0)
        memset(sch_p[:, :, :], 0.0)
        memset(ffnh_p[:, :, :, :], 0.0)
        memset(S32[:, 0, :, :], 0.0)
        memset(Sbf[:, 0, :, :], 0.0)
        lg = View(vec.res, vec.h[:, VOFF["lb_logits"]:VOFF["lb_logits"] + 32].rearrange("p (l c) -> p c l", l=4))
        P.add("dve", lambda e: e.tensor_reduce(out=lbs.h[:, :], in_=lg.ap, op=ALU.max, axis=mybir.AxisListType.X),
              reads=[lg], writes=[lbs[:, :]])
        for l in range(4):
            tt(lbt[:, :, l], vec[:, VOFF["lb_logits"] + 8 * l:VOFF["lb_logits"] + 8 * l + 8], lbs[:, :], ALU.subtract)
        act(lbt[:, :, :], lbt[:, :, :], AF.Exp)
        P.add("dve", lambda e: e.tensor_reduce(out=lbs.h[:, :], in_=lbt.h[:, :, :], op=ALU.add, axis=mybir.AxisListType.X),
              reads=[lbt[:, :, :]], writes=[lbs[:, :]])
        P.add("dve", lambda e: e.reciprocal(out=lbs.h[:, :], in_=lbs.h[:, :]), reads=[lbs[:, :]], writes=[lbs[:, :]])
        tt(omlv[:, :], lbt[:, :, 0], lbs[:, :], ALU.mult)
        ts(lbv[:, :], omlv[:, :], -1.0, 1.0, ALU.mult, ALU.add)
        ts(nomlv[:, :], omlv[:, :], -0.5, None, ALU.mult)
        ts(homlv[:, :], omlv[:, :], 0.5, None, ALU.mult)
        tt(lbhv[:, :], lbv[:, :], homlv[:, :], ALU.add)
        for t_ in (vec, ident, identb, onesb, mask4, rmask, invc):
            pass

        def pbank(b, n, c0=0):
            return pst[:, b, c0:c0 + n]

        def load_fm(dram2d, R, W, dst_fn):
            w0 = 0
            while w0 < W:
                wl = min(1024, W - w0)
                s = nxt("stg")
                dma_in(stg[0:R, s, 0:wl], dram2d[:, w0:w0 + wl], "stg%d" % s)
                nch = wl // 128
                c = 0
                while c < nch:
                    k = min(4, nch - c)
                    b = bank()
                    for q in range(k):
                        tr(pst[:, b, q * 32:q * 32 + R], stg[0:R, s, (c + q) * 128:(c + q + 1) * 128], ident[0:R, 0:R])
                    src = View(pst.res, pst.h[:, b, 0:k * 32].rearrange("p (k r) -> p k r", k=k)[:, :, 0:R], pst.gran)
                    cp(dst_fn(w0 // 128 + c, k), src)
                    c += k
                w0 += wl

        def store_fm(src_fn, R, W, dram2d, key):
            w0 = 0
            while w0 < W:
                wl = min(1024, W - w0)
                s = nxt("stg")
                nch = wl // 128
                c = 0
                while c < nch:
                    k = min(4, nch - c)
                    b = bank()
                    for q in range(k):
                        tr(pst[0:R, b, q * 128:(q + 1) * 128], src_fn(w0 // 128 + c + q), ident[:, :])
                    cp(stg[0:R, s, c * 128:(c + k) * 128], pst[0:R, b, 0:k * 128])
                    c += k
                dma_out(dram2d[:, w0:w0 + wl], stg[0:R, s, 0:wl], key + str(s))
                w0 += wl

        nst = {"banks": None, "prev": [], "cur": []}

        def stats_begin():
            bs = []
            for _ in ntiles:
                b = bank()
                reserved.add(b)
                bs.append(b)
            nst["banks"] = bs
            nst["prev"] = []
            nst["cur"] = []

        def resid_done(m, ti, n0, n1):
            n = n1 - n0
            q = nxt("sqb", 6)
            act(sqb[:, q, 0:n], x[:, m, n0:n1], AF.Square)
            b = nst["banks"][ti]
            nst["cur"].append(lambda: mm(pbank(b, n), onesb[:, :], sqb[:, q, 0:n], m == 0, m == NC8 - 1))

        def stats_step():
            for fn in nst["prev"]:
                fn()
            nst["prev"] = nst["cur"]
            nst["cur"] = []

        def stats_end():
            for fn in nst["prev"] + nst["cur"]:
                fn()
            nst["prev"] = []
            nst["cur"] = []

        def rmsnorm(gname, gc0, out_fn, eng_sq="act", merge=True):
            pre = nst["banks"]
            nst["banks"] = None
            if pre is not None:
                for ti, (n0, n1) in enumerate(ntiles):
                    n = n1 - n0
                    b = pre[ti]
                    reserved.discard(b)
                    r = nxt("tmpa")
                    act(tmpa[:, r, 0:n], pbank(b, n), AF.Ln, bias=epsv[:, 0:1], scale=1.0 / D)
                    act(rstd[:, n0:n1], tmpa[:, r, 0:n], AF.Exp, scale=-0.5)
                if merge and len(ntiles) > 1:
                    ranges = [ntiles[0], (ntiles[1][0], T)]
                else:
                    ranges = list(ntiles)
                for (n0, n1) in ranges:
                    for c in range(NC8):
                        stt(out_fn(c, n0, n1), x[:, c, n0:n1], V(gname, gc0 + c), rstd[:, n0:n1], ALU.mult, ALU.mult)
                return
            for ti, (n0, n1) in enumerate(ntiles):
                n = n1 - n0
                if pre is None:
                    b = bank()
                    for c in range(NC8):
                        q = nxt("sqb", 6)
                        act(sqb[:, q, 0:n], x[:, c, n0:n1], AF.Square)
                        mm(pbank(b, n), onesb[:, :], sqb[:, q, 0:n], c == 0, c == NC8 - 1)
                else:
                    b = pre[ti]
                    reserved.discard(b)
                r = nxt("tmpa")
                act(tmpa[:, r, 0:n], pbank(b, n), AF.Ln, bias=epsv[:, 0:1], scale=1.0 / D)
                act(rstd[:, n0:n1], tmpa[:, r, 0:n], AF.Exp, scale=-0.5)
                for c in range(NC8):
                    stt(out_fn(c, n0, n1), x[:, c, n0:n1], V(gname, gc0 + c), rstd[:, n0:n1], ALU.mult, ALU.mult)

        epsv = sb("epsv", [128, 1], F32)
        memset(epsv[:, :], EPS)

        def proj(wb, part, c0, rhs_fn, KC, n0, n1):
            b = bank()
            n = n1 - n0
            for kc in range(KC):
                mm(pbank(b, n), wb.lhsT(part, kc, c0), rhs_fn(kc), kc == 0, kc == KC - 1)
            return pbank(b, n)

        def xn_rhs(n0, n1):
            return lambda kc: xn[:, kc, n0:n1]

        def carve(off_bf16, shape, dt):
            n = int(np.prod(shape))
            if dt == F32:
                assert off_bf16 % 2 == 0
                ap = scr.h[:, off_bf16:off_bf16 + 2 * n].bitcast(F32)
                nb = 2 * n
            else:
                ap = scr.h[:, off_bf16:off_bf16 + n]
                nb = n
            if len(shape) == 2:
                ap = ap.rearrange("p (a b) -> p a b", a=shape[0])
            elif len(shape) == 3:
                ap = ap.rearrange("p (a b c) -> p a b c", a=shape[0], b=shape[1])
            assert off_bf16 + nb <= NSCR, (off_bf16, nb)
            return scr.sub(ap), off_bf16 + nb

        def ffn(L, p, last):
            gu, dn = plan[p]["ffn"][L]
            h, _ = carve(0, [NF, T], BF16)
            rmsnorm("norm_ffn", 8 * L, lambda c, n0, n1: xn[:, c, n0:n1])
            wv = VOFF["ffn_wdw"] + L * 66
            pending = []
            for f in range(NF):
                wb = wget(gu[f // 2])
                c0 = (f % 2) * 128
                prev = None
                for (n0, n1) in ntiles:
                    n = n1 - n0
                    g = nxt("gbuf")
                    if n0 == 0:
                        cp(gbuf[:, g, 0:2], ffnh_p[:, L, f, :])
                    elif n0 == TP:
                        cp(gbuf[:, g, 0:2], ffnh_s[:, L, f, :])
                    else:
                        cp(gbuf[:, g, 0:2], gbuf[:, prev[0], prev[1]:prev[1] + 2])
                    pg = proj(wb, 0, c0, xn_rhs(n0, n1), 8, n0, n1)
                    pu = proj(wb, 1, c0, xn_rhs(n0, n1), 8, n0, n1)
                    cp(gbuf[:, g, 2:2 + n], pg)
                    a = nxt("accb")
                    act(accb[:, a, 0:n], pg, AF.Copy, scale=vec[:, wv + 44 + f:wv + 44 + f + 1])
                    stt(accb[:, a, 0:n], gbuf[:, g, 0:n], vec[:, wv + f:wv + f + 1],
                        accb[:, a, 0:n], ALU.mult, ALU.add)
                    stt(accb[:, a, 0:n], gbuf[:, g, 1:1 + n], vec[:, wv + 22 + f:wv + 22 + f + 1],
                        accb[:, a, 0:n], ALU.mult, ALU.add)
                    prev = (g, n)
                    if n1 == TP:
                        cp(ffnh_p[:, L, f, :], gbuf[:, g, n:n + 2])
                    if n1 == T:
                        cp(ffnt_s[:, f, :], gbuf[:, g, n:n + 2])
                    for fn in pending:
                        fn()
                    pending.clear()

                    def stage_b(a=a, n=n, f=f, n0=n0, n1=n1, pu=pu):
                        s_ = nxt("silb")
                        act(silb[:, s_, 0:n], accb[:, a, 0:n], AF.Silu, bias=V("ffn_bdw", L * 22 + f))
                        tt(h[:, f, n0:n1], silb[:, s_, 0:n], pu, ALU.mult)
                    pending.append(stage_b)
            for fn in pending:
                fn()
            pending.clear()
            stats_begin()
            for m in range(NC8):
                wb = wget(dn[m // 2])
                c0 = (m % 2) * 128
                for ti, (n0, n1) in enumerate(ntiles):
                    pd = proj(wb, 0, c0, lambda kc, n0=n0, n1=n1: h[:, kc, n0:n1], NF, n0, n1)
                    tt(x[:, m, n0:n1], pd, x[:, m, n0:n1], ALU.add)
                    resid_done(m, ti, n0, n1)
                stats_step()
            stats_end()
            store_fm(lambda c: ffnt_s[:, c, :], 2, DFF, offn_s[L, p], "ost")
            if last:
                store_fm(lambda c: ffnh_p[:, L, c, :], 2, DFF, offn_p[L], "ost")

        ND_TAPS = 6

        def conformer(p, last):
            d = plan[p]
            EXT = T + 60
            cf, o = carve(0, [NC8, T], F32)
            diag, o = carve(o, [2, 31, 128], BF16)
            ub, o = carve(o, [3, EXT], BF16)
            rmsnorm("norm_mix", 0, lambda c, n0, n1: xn[:, c, n0:n1])
            def stage_a(m):
                wb = wget(d["w1"][m // 2])
                c0 = (m % 2) * 128
                u = nxt("ub", 3)
                dg = nxt("diag")
                for j in range(ND_TAPS, 31):
                    ts(diag[:, dg, j, :], identb[:, :], V("conf_wdw", j * 8 + m), None, ALU.mult)
                cp(ub[:, u, 0:30], confh_p[:, m, :])
                cp(ub[:, u, TP + 30:TP + 60], confh_s[:, m, :])
                for (n0, n1) in ntiles:
                    n = n1 - n0
                    pa = proj(wb, 0, c0, xn_rhs(n0, n1), 8, n0, n1)
                    pg = proj(wb, 1, c0, xn_rhs(n0, n1), 8, n0, n1)
                    r = nxt("tmpa")
                    act(tmpa[:, r, 0:n], pg, AF.Sigmoid, bias=V("conf_b1", 8 + m))
                    r2 = nxt("tmpb")
                    stt(tmpb[:, r2, 0:n], pa, V("conf_b1", m), tmpa[:, r, 0:n], ALU.add, ALU.mult)
                    e0 = ecol(n0, 30)
                    cp(ub[:, u, e0:e0 + n], tmpb[:, r2, 0:n])
                    seg = 0 if n0 < TP else 1
                    send = TP if seg == 0 else T
                    if n1 == send:
                        cp(utail[:, seg, m, :], tmpb[:, r2, n - 32:n], eng="pool")
                cp(confh_p[:, m, :], ub[:, u, TP:TP + 30])
                return (u, dg)

            def stage_b(m, u, dg):
                assert len(ntiles) <= 4
                accs = [accb[:, 0, :], accb[:, 1, :], silb[:, 0, :], silb[:, 1, :]]
                for j in range(ND_TAPS):
                    for ti, (n0, n1) in enumerate(ntiles):
                        n = n1 - n0
                        e0 = ecol(n0, 30) - 30
                        acc = View(accs[ti].res, accs[ti].ap[:, 0:n])
                        src = ub[:, u, e0 + j:e0 + j + n]
                        if j == 0:
                            ts(acc, src, V("conf_wdw", j * 8 + m), None, ALU.mult)
                        else:
                            stt(acc, src, V("conf_wdw", j * 8 + m), acc, ALU.mult, ALU.add)
                for ti, (n0, n1) in enumerate(ntiles):
                    n = n1 - n0
                    e0 = ecol(n0, 30) - 30
                    acc = View(accs[ti].res, accs[ti].ap[:, 0:n])
                    b = bank()
                    for j in range(ND_TAPS, 31):
                        mm(pbank(b, n), diag[:, dg, j, :], ub[:, u, e0 + j:e0 + j + n], j == ND_TAPS, j == 30)
                    stt(cf[:, m, n0:n1], pbank(b, n), V("conf_bdw", m), acc, ALU.add, ALU.add)

            st = {}
            st[0] = stage_a(0)
            for m in range(NC8):
                if m + 1 < NC8:
                    st[m + 1] = stage_a(m + 1)
                stage_b(m, *st[m])
            for (n0, n1) in ntiles:
                n = n1 - n0
                b1 = bank()
                b2 = bank()
                for c in range(NC8):
                    q = nxt("sqb", 6)
                    cp(sqb[:, q, 0:n], cf[:, c, n0:n1])
                    mm(pbank(b1, n), onesb[:, :], sqb[:, q, 0:n], c == 0, c == NC8 - 1)
                    q = nxt("sqb", 6)
                    act(sqb[:, q, 0:n], cf[:, c, n0:n1], AF.Square)
                    mm(pbank(b2, n), onesb[:, :], sqb[:, q, 0:n], c == 0, c == NC8 - 1)
                r = nxt("tmpa")
                mean = tmpa[:, r, 0:n]
                act(mean, pbank(b1, n), AF.Copy, scale=1.0 / D)
                r2 = nxt("tmpb")
                msq = tmpb[:, r2, 0:n]
                tt(msq, mean, mean, ALU.mult)
                r3 = nxt("silb")
                var = silb[:, r3, 0:n]
                stt(var, pbank(b2, n), 1.0 / D, msq, ALU.mult, ALU.subtract)
                act(var, var, AF.Ln, bias=epsv[:, 0:1])
                act(rstd[:, n0:n1], var, AF.Exp, scale=-0.5)
                for c in range(NC8):
                    a = nxt("accb")
                    tt(accb[:, a, 0:n], cf[:, c, n0:n1], mean, ALU.subtract)
                    tt(accb[:, a, 0:n], accb[:, a, 0:n], rstd[:, n0:n1], ALU.mult)
                    act(xn[:, c, n0:n1], accb[:, a, 0:n], AF.Silu, bias=V("conf_lnb", c), scale=V("conf_lng", c))
            stats_begin()
            for m in range(NC8):
                wb = wget(d["w2"][m // 4])
                c0 = (m % 4) * 128
                for ti, (n0, n1) in enumerate(ntiles):
                    pd = proj(wb, 0, c0, xn_rhs(n0, n1), 8, n0, n1)
                    stt(x[:, m, n0:n1], pd, V("conf_b2", m), x[:, m, n0:n1], ALU.add, ALU.add)
                    resid_done(m, ti, n0, n1)
                stats_step()
            stats_end()
            store_fm(lambda c: utail[:, 1, c, 2:32], 30, D, oconf_s[p], "ost")
            if last:
                store_fm(lambda c: utail[:, 0, c, 2:32], 30, D, oconf_p, "ost")

        def poolmix(p, last):
            d = plan[p]
            EXT = T + 30
            xe, o = carve(0, [NC8, EXT], F32)
            tsb, o = carve(o, [4, EXT], F32)
            dfb = xn
            memset(tsb[:, :, 0:16], 0.0)
            for c in range(NC8):
                cp(xe[:, c, 0:15], poolh_p[:, c, :], eng="pool")
                cp(xe[:, c, TP + 15:TP + 30], poolh_s[:, c, :], eng="pool")
            rmsnorm("norm_mix", 8, lambda c, n0, n1: xe[:, c, ecol(n0, 15):ecol(n0, 15) + (n1 - n0)], merge=False)
            for c in range(NC8):
                cp(poolh_p[:, c, :], xe[:, c, TP:TP + 15], eng="pool")
            wb = wget(d["pool"])
            stats_begin()
            for g in range(4):
                w = (2, 4, 8, 16)[g]
                for c in (2 * g, 2 * g + 1):
                    on_pool = c >= 4
                    sh = 1
                    cur = None
                    while sh < w:
                        a = nxt("tsbP" if on_pool else "tsbD") + (2 if on_pool else 0)
                        srcv = (lambda lo, hi, c=c: xe[:, c, lo:hi]) if cur is None else (lambda lo, hi, cur=cur: tsb[:, cur, lo:hi])
                        tt(tsb[:, a, sh:EXT], srcv(sh, EXT), srcv(0, EXT - sh), ALU.add, eng=("pool" if on_pool else "dve"))
                        cur = a
                        sh *= 2
                    for seg, (t0, n) in enumerate(((0, TP), (TP, TS))):
                        e0 = ecol(t0, 15)
                        stt(dfb[:, c, t0:t0 + n], tsb[:, cur, e0:e0 + n], 1.0 / w, xe[:, c, e0:e0 + n], ALU.mult, ALU.subtract)
                    if p == 0:
                        r = nxt("tmpa")
                        tt(tmpa[:, r, 0:16], tsb[:, cur, 15:31], invc[:, g * 16:(g + 1) * 16], ALU.mult)
                        tt(dfb[:, c, 0:16], tmpa[:, r, 0:16], xe[:, c, 15:31], ALU.subtract)
                for m in (2 * g, 2 * g + 1):
                    for ti, (n0, n1) in enumerate(ntiles):
                        n = n1 - n0
                        b = bank()
                        for kc in range(2):
                            mm(pbank(b, n), wb.lhsT(g, kc, (m % 2) * 128), dfb[:, 2 * g + kc, n0:n1], kc == 0, kc == 1)
                        stt(x[:, m, n0:n1], pbank(b, n), V("pool_scale", m), x[:, m, n0:n1], ALU.mult, ALU.add)
                        resid_done(m, ti, n0, n1)
                    stats_step()
            stats_end()
            store_fm(lambda c: xe[:, c, T + 15:T + 30], 15, D, opool_s[p], "ost")
            if last:
                store_fm(lambda c: poolh_p[:, c, :], 15, D, opool_p, "ost")

        def sconv(p, last):
            d = plan[p]
            EXT = T + 4
            gt, o = carve(0, [NC8, T], BF16)
            pv, o = carve(o, [2, EXT], F32)
            bgb, o = carve(o, [2, T], F32)
            rmsnorm("norm_mix", 16, lambda c, n0, n1: xn[:, c, n0:n1])
            wv = VOFF["sconv_wdw"]
            for m in range(NC8):
                wb = wget(d["swin"][m])
                c0 = 0
                q = nxt("pv")
                cp(pv[:, q, 0:2], sch_p[:, m, :], eng="pool")
                cp(pv[:, q, TP + 2:TP + 4], sch_s[:, m, :], eng="pool")
                for (n0, n1) in ntiles:
                    n = n1 - n0
                    e0 = ecol(n0, 2)
                    pbg = proj(wb, 0, c0, xn_rhs(n0, n1), 8, n0, n1)
                    pcg = proj(wb, 1, c0, xn_rhs(n0, n1), 8, n0, n1)
                    pvv = proj(wb, 2, c0, xn_rhs(n0, n1), 8, n0, n1)
                    r = nxt("tmpa")
                    cp(tmpa[:, r, 0:n], pcg)
                    tt(pv[:, q, e0:e0 + n], tmpa[:, r, 0:n], pvv, ALU.mult)
                    cp(bgb[:, q, n0:n1], pbg)
                    a = nxt("accb")
                    ts(accb[:, a, 0:n], pv[:, q, e0 - 2:e0 - 2 + n], vec[:, wv + m:wv + m + 1], None, ALU.mult)
                    stt(accb[:, a, 0:n], pv[:, q, e0 - 1:e0 - 1 + n], vec[:, wv + 8 + m:wv + 8 + m + 1],
                        accb[:, a, 0:n], ALU.mult, ALU.add)
                    stt(accb[:, a, 0:n], pv[:, q, e0:e0 + n], vec[:, wv + 16 + m:wv + 16 + m + 1],
                        accb[:, a, 0:n], ALU.mult, ALU.add)
                    tt(gt[:, m, n0:n1], accb[:, a, 0:n], bgb[:, q, n0:n1], ALU.mult)
                cp(sch_p[:, m, :], pv[:, q, TP:TP + 2], eng="pool")
                cp(sch_s[:, m, :], pv[:, q, T + 2:T + 4], eng="pool")
            stats_begin()
            for m in range(NC8):
                wb = wget(d["swout"][m // 4])
                c0 = (m % 4) * 128
                for ti, (n0, n1) in enumerate(ntiles):
                    pd = proj(wb, 0, c0, lambda kc, n0=n0, n1=n1: gt[:, kc, n0:n1], 8, n0, n1)
                    tt(x[:, m, n0:n1], pd, x[:, m, n0:n1], ALU.add)
                    resid_done(m, ti, n0, n1)
                stats_step()
            stats_end()
            store_fm(lambda c: sch_s[:, c, :], 2, D, osc_s[p], "ost")
            if last:
                store_fm(lambda c: sch_p[:, c, :], 2, D, osc_p, "ost")

        def hgrn(p, last):
            d = plan[p]
            NGR = len(groups)
            og, o = carve(0, [NC8, T], BF16)
            qd, o = carve(o, [2, T], BF16)
            kd, o = carve(o, [2, T], BF16)
            ovf = o
            vf, o = carve(o, [2, T], BF16)
            vf32 = scr.sub(scr.h[:, ovf:ovf + 2 * T].bitcast(F32))

            def gsl(hb, n0, n1):
                return rstd[:, n0:n1] if hb == 0 else vf32[:, n0:n1]
            kdt, o = carve(o, [2, NGR, 128], BF16)
            vtk, o = carve(o, [2, NGR, 128], BF16)
            scm, o = carve(o, [2, NGR, 128], BF16)
            o32, o = carve(o, [2, T], F32)
            bcu, o = carve(o, [2, 512], F32)
            rmsnorm("norm_mix", 24, lambda c, n0, n1: xn[:, c, n0:n1])
            pstb = pst.sub(pst.h[:, :, :].bitcast(BF16))
            for pr in range(4):
                pair = (2 * pr, 2 * pr + 1)
                wb = wget(d["hqf"][pr])
                for (n0, n1) in ntiles:
                    n = n1 - n0
                    info = {}
                    for h in pair:
                        c0 = (h % 2) * 128
                        pq = proj(wb, 0, c0, xn_rhs(n0, n1), 8, n0, n1)
                        pf = proj(wb, 1, c0, xn_rhs(n0, n1), 8, n0, n1)
                        rq = nxt("silb")
                        act(silb[:, rq, 0:n], pq, AF.Silu)
                        rs = nxt("tmpa")
                        act(tmpa[:, rs, 0:n], pf, AF.Tanh, scale=0.5)
                        info[h] = [rq, rs]
                    for h in pair:
                        rq, rs = info[h]
                        rl = nxt("tmpb")
                        act(tmpb[:, rl, 0:n], tmpa[:, rs, 0:n], AF.Ln, bias=lbhv[:, h:h + 1], scale=homlv[:, h:h + 1])
                        r = nxt("hr")
                        P.add("dve", lambda e, r=r, n=n, rl=rl: e.tensor_tensor_scan(
                            out=bcu.h[:, r, 0:n], data0=rmask.h[:, 0:n], data1=tmpb.h[:, rl, 0:n], initial=0.0,
                            op0=ALU.mult, op1=ALU.add),
                            reads=[rmask[:, 0:n], tmpb[:, rl, 0:n]], writes=[bcu[:, r, 0:n]])
                        rk = nxt("accb")
                        ts(accb[:, rk, 0:n], tmpa[:, rs, 0:n], nomlv[:, h:h + 1], homlv[:, h:h + 1], ALU.mult, ALU.add)
                        info[h] += [rl, rk, r]
                    for h in pair:
                        hb = h % 2
                        rq, rs, rl, rk, r = info[h]
                        act(tmpa[:, rs, 0:n], bcu[:, r, 0:n], AF.Exp)
                        act(tmpb[:, rl, 0:n], bcu[:, r, 0:n], AF.Exp, scale=-1.0)
                        tt(qd[:, hb, n0:n1], silb[:, rq, 0:n], tmpa[:, rs, 0:n], ALU.mult)
                        tt(kd[:, hb, n0:n1], accb[:, rk, 0:n], tmpb[:, rl, 0:n], ALU.mult)
                        for gi, (t0, ntok, seg) in enumerate(groups):
                            if n0 <= t0 < n1:
                                le = t0 + ntok - 1 - n0
                                cp(eblast[:, hb, gi:gi + 1], tmpa[:, rs, le:le + 1], eng="pool")
                wg = wget(d["hg"][pr])
                for h in pair:
                    hb = h % 2
                    for (n0, n1) in ntiles:
                        pvv = proj(wg, 0, (h % 2) * 128, xn_rhs(n0, n1), 8, n0, n1)
                        cp(vf[:, hb, n0:n1], pvv)
                for gi, (t0, ntok, seg) in enumerate(groups):
                    for h in pair:
                        hb = h % 2
                        b = bank()
                        tr(pstb[0:ntok, b, 0:128], kd[:, hb, t0:t0 + ntok], identb[:, :])
                        tr(pstb[0:ntok, b, 128:256], vf[:, hb, t0:t0 + ntok], identb[:, :])
                        cp(kdt[0:ntok, hb, gi, :], pstb[0:ntok, b, 0:128])
                        cp(vtk[0:ntok, hb, gi, :], pstb[0:ntok, b, 128:256], eng="dve")
                for gi, (t0, ntok, seg) in enumerate(groups):
                    for h in pair:
                        hb = h % 2
                        b = bank()
                        mm(pst[0:ntok, b, 0:ntok], kd[:, hb, t0:t0 + ntok], qd[:, hb, t0:t0 + ntok], True, True)
                        tt(scm[0:ntok, hb, gi, 0:ntok], pst[0:ntok, b, 0:ntok], mask4[0:ntok, 0:ntok], ALU.mult)
                gsteps = []
                for h in pair:
                    for (n0, n1) in ntiles:
                        def gst(h=h, n0=n0, n1=n1):
                            pgt = proj(wg, 1, (h % 2) * 128, xn_rhs(n0, n1), 8, n0, n1)
                            act(gsl(h % 2, n0, n1), pgt, AF.Silu)
                        gsteps.append(gst)
                nstep = 0
                for gi, (t0, ntok, seg) in enumerate(groups):
                    for h in pair:
                        hb = h % 2
                        b3 = bank()
                        mm(pst[:, b3, 0:128], kdt[0:ntok, hb, gi, :], vtk[0:ntok, hb, gi, :], True, True)
                        b2 = bank()
                        mm(pst[:, b2, 0:ntok], vtk[0:ntok, hb, gi, :], scm[0:ntok, hb, gi, 0:ntok], True, False)
                        mm(pst[:, b2, 0:ntok], Sbf[:, seg, h, :], qd[:, hb, t0:t0 + ntok], False, True)
                        cp(o32[:, hb, t0:t0 + ntok], pst[:, b2, 0:ntok])
                        ts(Setmp[:, hb, :], S32[:, seg, h, :], eblast[:, hb, gi:gi + 1], None, ALU.mult)
                        stt(Sbf[:, seg, h, :], pst[:, b3, 0:128], eblast[:, hb, gi:gi + 1], Setmp[:, hb, :],
                            ALU.mult, ALU.add)
                        stt(S32[:, seg, h, :], pst[:, b3, 0:128], eblast[:, hb, gi:gi + 1], Setmp[:, hb, :],
                            ALU.mult, ALU.add)
                        nstep += 1
                        if nstep % 3 == 0 and gsteps:
                            gsteps.pop(0)()
                for gst in gsteps:
                    gst()
                for h in pair:
                    hb = h % 2
                    for (n0, n1) in ntiles:
                        n = n1 - n0
                        q = nxt("sqb", 6)
                        tt(sqb[:, q, 0:n], o32[:, hb, n0:n1], o32[:, hb, n0:n1], ALU.mult, eng="pool")
                        b = bank()
                        mm(pbank(b, n), onesb[:, :], sqb[:, q, 0:n], True, True)
                        r = nxt("tmpa")
                        act(tmpa[:, r, 0:n], pbank(b, n), AF.Ln, bias=epsv[:, 0:1], scale=1.0 / 128)
                        act(tmpa[:, r, 0:n], tmpa[:, r, 0:n], AF.Exp, scale=-0.5)
                        a = nxt("accb")
                        stt(accb[:, a, 0:n], o32[:, hb, n0:n1], V("hgrn_ng", 0), tmpa[:, r, 0:n], ALU.mult, ALU.mult)
                        tt(og[:, h, n0:n1], accb[:, a, 0:n], gsl(hb, n0, n1), ALU.mult)
            stats_begin()
            for m in range(NC8):
                wb = wget(d["ho"][m // 4])
                c0 = (m % 4) * 128
                for ti, (n0, n1) in enumerate(ntiles):
                    pd = proj(wb, 0, c0, lambda kc, n0=n0, n1=n1: og[:, kc, n0:n1], 8, n0, n1)
                    tt(x[:, m, n0:n1], pd, x[:, m, n0:n1], ALU.add)
                    resid_done(m, ti, n0, n1)
                stats_step()
            stats_end()
            for h in range(8):
                dma_out(ohg_s[p, h], S32[:, 1, h, :], "ohg")
                if last:
                    dma_out(ohg_p[h], S32[:, 0, h, :], "ohg")

        NL = 4
        lslots = [scr.sub(scr.h[:, 17408 + 2048 * k:17408 + 2048 * (k + 1)].bitcast(F32)) for k in range(NL)]
        assert 17408 + 2048 * NL <= NSCR

        def xload_dma(p, gi):
            t0, ntok, seg = groups[gi]
            src = xp_d[p * TP + t0:p * TP + t0 + 128, :] if seg == 0 else xs_d[p]
            k = gi % NL
            dma_in(lslots[k][0:ntok, :], src, "lst%d" % k)

        def xload_compute(p, gi):
            t0, ntok, seg = groups[gi]
            sl = lslots[gi % NL]
            for half in range(2):
                b = bank()
                for q in range(4):
                    c = half * 4 + q
                    tr(pst[:, b, q * 128:q * 128 + ntok], sl[0:ntok, c * 128:(c + 1) * 128], ident[0:ntok, 0:ntok])
                srcv = View(pst.res, pst.h[:, b, :].rearrange("p (k r) -> p k r", k=4)[:, :, 0:ntok], pst.gran)
                cp(x[:, half * 4:half * 4 + 4, t0:t0 + ntok], srcv, eng=("act" if half == 0 else "dve"))

        for p in range(n_pass):
            last = p == n_pass - 1
            if p == 0:
                for gi in range(min(NL, len(groups))):
                    xload_dma(0, gi)
                for gi in range(len(groups)):
                    xload_compute(0, gi)
                    if gi + NL < len(groups):
                        xload_dma(0, gi + NL)
            load_fm(sconf_d[p], 30, D, lambda c, k: confh_s[:, c:c + k, :])
            load_fm(spool_d[p], 15, D, lambda c, k: poolh_s[:, c:c + k, :])
            load_fm(ssc_d[p], 2, D, lambda c, k: sch_s[:, c:c + k, :])
            for L in range(4):
                load_fm(sffn_d[L, p], 2, DFF, lambda c, k, L=L: ffnh_s[:, L, c:c + k, :])
            for h in range(8):
                dma_in(S32[:, 1, h, :], shg_d[p, h], "shg")
            cp(Sbf[:, 1, :, :], S32[:, 1, :, :], eng="pool")
            conformer(p, last)
            ffn(0, p, last)
            poolmix(p, last)
            ffn(1, p, last)
            sconv(p, last)
            ffn(2, p, last)
            hgrn(p, last)
            ffn(3, p, last)
            xo, _ = carve(0, [NC8, T], F32)
            rmsnorm("norm_final", 0, lambda c, n0, n1: xo[:, c, n0:n1])
            if not last:
                for gi in range(min(NL, len(groups))):
                    xload_dma(p + 1, gi)
            for gi, (t0, ntok, seg) in enumerate(groups):
                s = nxt("stg")
                for half in range(2):
                    b = bank()
                    for q in range(4):
                        c = half * 4 + q
                        tr(pst[0:ntok, b, q * 128:(q + 1) * 128], xo[:, c, t0:t0 + ntok], ident[:, :])
                    cp(stg[0:ntok, s, half * 512:(half + 1) * 512], pst[0:ntok, b, :], eng=("act" if half == 0 else "dve"))
                dst = yp_d[p * TP + t0:p * TP + t0 + 128, :] if seg == 0 else ys_d[p]
                dma_out(dst, stg[0:ntok, s, :], "oy%d" % s)
                if not last:
                    xload_compute(p + 1, gi)
                    if gi + NL < len(groups):
                        xload_dma(p + 1, gi + NL)
        P.emit(es)
    return nc


N_CORES = 8
N_PASS = 4
TP_FULL = 1024
_WNAMES = ["conf_w_pw1", "conf_w_pw2", "pool_w", "sconv_w_in", "sconv_w_out", "hgrn_w_q", "hgrn_w_f",
           "hgrn_w_i", "hgrn_w_g", "hgrn_w_o", "ffn_w_gate", "ffn_w_up", "ffn_w_down"]


def make_in_maps(inp, n_cores, n_pass, TP):
    T = TP + 64
    vec = build_vec(inp)
    ident, mask4, rmask, invc = build_consts(TP, T)
    shared = {"vec": vec, "c_ident": ident, "c_mask4": mask4, "c_rmask": rmask, "c_invc": invc}
    for n in _WNAMES:
        a = np.asarray(inp[n], np.float32)
        shared[n] = np.ascontiguousarray(a[0] if n in ("conf_w_pw1", "conf_w_pw2", "pool_w", "sconv_w_in", "sconv_w_out",
                                                        "hgrn_w_q", "hgrn_w_f", "hgrn_w_i", "hgrn_w_g", "hgrn_w_o") else a)
    maps = []
    for c in range(n_cores):
        sl = slice(c * n_pass, (c + 1) * n_pass)
        m = dict(shared)
        m["xp"] = np.ascontiguousarray(inp["x_prompt"][c], np.float32)
        m["xs"] = np.ascontiguousarray(inp["x_sample"][sl], np.float32)
        m["s_conf"] = np.ascontiguousarray(inp["state_conformer_conv"][0, sl], np.float32)
        m["s_pool"] = np.ascontiguousarray(inp["state_pool"][0, sl], np.float32)
        m["s_sconv"] = np.ascontiguousarray(inp["state_short_conv"][0, sl], np.float32)
        m["s_hgrn"] = np.ascontiguousarray(inp["state_hgrn"][0, sl], np.float32)
        m["s_ffn"] = np.ascontiguousarray(inp["state_ffn_conv"][:, sl], np.float32)
        maps.append(m)
    return maps


def gather(results, n_cores, n_pass):
    def cat(k, axis=0):
        return np.concatenate([np.asarray(r[k], np.float32) for r in results], axis=axis)

    def stk(k):
        return np.stack([np.asarray(r[k], np.float32) for r in results], axis=0)
    y_p = stk("yp")
    y_s = cat("ys")
    return (y_p, y_s,
            stk("o_conf_p")[None], cat("o_conf_s")[None],
            stk("o_pool_p")[None], cat("o_pool_s")[None],
            stk("o_sconv_p")[None], cat("o_sconv_s")[None],
            stk("o_hgrn_p")[None], cat("o_hgrn_s")[None],
            np.stack([np.asarray(r["o_ffn_p"], np.float32) for r in results], axis=1),
            np.concatenate([np.asarray(r["o_ffn_s"], np.float32) for r in results], axis=1))


def kernel(**inputs):
    inp = {k: np.asarray(v) for k, v in inputs.items()}
    nc = build(N_PASS, TP_FULL)
    maps = make_in_maps(inp, N_CORES, N_PASS, TP_FULL)
    res = run_bass_kernel_spmd(nc, maps, core_ids=list(range(N_CORES)))
    return gather(res.results, N_CORES, N_PASS)
```

```python
import numpy as np
from contextlib import ExitStack
import concourse.bass as bass
import concourse.mybir as mybir
from concourse.bass_utils import run_bass_kernel_spmd

F32 = mybir.dt.float32
BF16 = mybir.dt.bfloat16
AF = mybir.ActivationFunctionType
ALU = mybir.AluOpType

D = 1024
DFF = 2816
NC8 = 8
NF = 22
EPS = 1e-6
PAST_LEN = 2048


def _isz(dt):
    return 2 if dt == BF16 else 4


class Res:
    __slots__ = ("name", "acc", "frozen")

    def __init__(self, name):
        self.name = name
        self.acc = []
        self.frozen = False


class Op:
    __slots__ = ("eng", "fn", "deps", "sig", "cnt", "dma", "dsem", "dval", "dgrp")

    def __init__(self, eng, fn, dma=False):
        self.eng = eng
        self.fn = fn
        self.deps = []
        self.sig = False
        self.cnt = 0
        self.dma = dma
        self.dsem = None
        self.dval = 0
        self.dgrp = None


class View:
    __slots__ = ("res", "ap", "lo", "hi")

    def __init__(self, res, ap, gran=1):
        self.res = res
        self.ap = ap
        a = ap.ap
        isz = _isz(ap.dtype)
        pstep = a[0][0]
        lo = ap.offset % pstep if pstep > 0 else ap.offset
        ext = 1
        for (st, cn) in a[1:]:
            ext += (cn - 1) * abs(st)
        lo *= isz
        hi = lo + ext * isz
        if gran > 1:
            lo = (lo // gran) * gran
            hi = -((-hi) // gran) * gran
        self.lo = lo
        self.hi = hi

    @property
    def iv(self):
        return (self.res, self.lo, self.hi)


class TT:
    def __init__(self, name, ap_or_handle, gran=1, res=None):
        self.res = res if res is not None else Res(name)
        self.h = ap_or_handle
        self.gran = gran

    def __getitem__(self, idx):
        return View(self.res, self.h[idx], self.gran)

    def sub(self, ap):
        return TT(self.res.name, ap, self.gran, self.res)

    def freeze(self):
        self.res.frozen = True


class Prog:
    ENGS = ("pe", "act", "dve", "pool", "sp")

    def __init__(self, nc):
        self.nc = nc
        self.ops = {e: [] for e in self.ENGS}
        self.dma_keys = {}

    def add(self, eng, fn, reads=(), writes=(), dma_key=None):
        op = Op(eng, fn, dma=dma_key is not None)
        if dma_key is not None:
            ent = self.dma_keys.setdefault(dma_key, [None, 0, None])
            ent[1] += 16
            op.dsem = dma_key
            op.dval = ent[1]
            if ent[2] is None:
                ent[2] = [0]
            ent[2][0] = ent[1]
            op.dgrp = ent[2]
        deps = op.deps
        for v in reads:
            res, lo, hi = v.res, v.lo, v.hi
            for a in res.acc:
                if a[3] and a[0] < hi and lo < a[1]:
                    self._dep(op, a[2], deps)
            if not res.frozen:
                if not op.dma:
                    for a in res.acc:
                        if (not a[3]) and a[0] == lo and a[1] == hi and a[2].eng == eng and not a[2].dma:
                            a[2] = op
                            break
                    else:
                        res.acc.append([lo, hi, op, False])
                else:
                    res.acc.append([lo, hi, op, False])
        for v in writes:
            res, lo, hi = v.res, v.lo, v.hi
            assert not res.frozen, res.name
            keep = []
            for a in res.acc:
                if a[0] < hi and lo < a[1]:
                    if a[2] is not op:
                        self._dep(op, a[2], deps)
                    if lo <= a[0] and a[1] <= hi:
                        continue
                keep.append(a)
            keep.append([lo, hi, op, True])
            res.acc = keep
        self.ops[eng].append(op)
        return op

    def _dep(self, op, prod, deps):
        if prod is op:
            return
        if (not prod.dma) and prod.eng == op.eng and op.eng == "pe" and not op.dma:
            return
        if prod not in deps:
            deps.append(prod)
            if not prod.dma:
                prod.sig = True
            else:
                ent = self.dma_keys[prod.dsem]
                if ent[2] is prod.dgrp:
                    ent[2] = None

    def emit(self, es, final_wait_eng="sp"):
        nc = self.nc
        esem = {e: es.enter_context(nc.semaphore("s_" + e)) for e in self.ENGS}
        for k, ent in self.dma_keys.items():
            ent[0] = es.enter_context(nc.semaphore("d_" + str(k)))
        for e in self.ENGS:
            c = 0
            for op in self.ops[e]:
                if op.sig and not op.dma:
                    c += 1
                    op.cnt = c
        block = es.enter_context(nc.Block())
        engobj = {"pe": "tensor", "act": "scalar", "dve": "vector", "pool": "gpsimd", "sp": "sync"}
        dma_keys = self.dma_keys

        def run(e, eng):
            known = {}
            for op in self.ops[e]:
                need = {}
                for p in op.deps:
                    if p.dma:
                        k = ("d", p.dsem)
                        s = dma_keys[p.dsem][0]
                        v = p.dgrp[0]
                    else:
                        k = ("e", p.eng)
                        s = esem[p.eng]
                        v = p.cnt
                    if known.get(k, 0) >= v:
                        continue
                    if k not in need or need[k][1] < v:
                        need[k] = (s, v)
                for k, (s, v) in need.items():
                    eng.wait_ge(s, v)
                    known[k] = v
                ins = op.fn(eng)
                if op.dma:
                    ins.then_inc(dma_keys[op.dsem][0], 16)
                elif op.sig:
                    ins.then_inc(esem[e], 1)
            if e == final_wait_eng:
                for k, ent in dma_keys.items():
                    if known.get(("d", k), 0) < ent[1]:
                        eng.wait_ge(ent[0], ent[1])

        for e in self.ENGS:
            if not self.ops[e] and e != final_wait_eng:
                continue
            deco = getattr(block, engobj[e])

            def mk(e=e):
                def body(eng):
                    run(e, eng)
                return body
            deco(mk())


def _vec_layout():
    off = {}
    c = 0
    for name, n in [("norm_mix", 32), ("norm_ffn", 32), ("norm_final", 8), ("conf_b1", 16),
                    ("conf_wdw", 248), ("conf_bdw", 8), ("conf_lng", 8), ("conf_lnb", 8),
                    ("conf_b2", 8), ("pool_scale", 8), ("sconv_wdw", 24), ("lb_logits", 32),
                    ("hgrn_ng", 1), ("ffn_wdw", 264), ("ffn_bdw", 88)]:
        off[name] = c
        c += n
    return off, c


VOFF, NVEC = _vec_layout()


def _fm(v):
    v = np.asarray(v, np.float32)
    return np.ascontiguousarray(v.reshape(-1, 128).T)


def build_vec(inp):
    cols = []
    cols.append(np.concatenate([_fm(inp["norm_mix"][i]) for i in range(4)], axis=1))
    cols.append(np.concatenate([_fm(inp["norm_ffn"][i]) for i in range(4)], axis=1))
    cols.append(_fm(inp["norm_final"]))
    cols.append(_fm(inp["conf_b_pw1"][0]))
    cols.append(np.concatenate([_fm(inp["conf_w_dw"][0][j]) for j in range(31)], axis=1))
    cols.append(_fm(inp["conf_b_dw"][0]))
    cols.append(_fm(inp["conf_ln_g"][0]))
    cols.append(_fm(inp["conf_ln_b"][0]))
    cols.append(_fm(inp["conf_b_pw2"][0]))
    cols.append(_fm(inp["pool_scale"][0]))
    cols.append(np.concatenate([_fm(inp["sconv_w_dw"][0][j]) for j in range(3)], axis=1))
    cols.append(np.concatenate([_fm(inp["hgrn_lb_logits"][i]) for i in range(4)], axis=1))
    cols.append(np.asarray(inp["hgrn_norm_g"][0], np.float32).reshape(128, 1))
    cols.append(np.concatenate([_fm(inp["ffn_w_dw"][i][j]) for i in range(4) for j in range(3)], axis=1))
    cols.append(np.concatenate([_fm(inp["ffn_b_dw"][i]) for i in range(4)], axis=1))
    v = np.concatenate(cols, axis=1)
    assert v.shape == (128, NVEC), v.shape
    return np.ascontiguousarray(v, np.float32)


def build_consts(TP, T):
    ident = np.eye(128, dtype=np.float32)
    s = np.arange(128)[:, None]
    t = np.arange(128)[None, :]
    mask = (t >= s).astype(np.float32)
    mask4 = mask
    rm = np.ones((512,), np.float32)
    rm[0:512:128] = 0.0
    rmask = np.tile(rm[None, :], (128, 1))
    invc = np.zeros((128, 4, 16), np.float32)
    for g, w in enumerate((2, 4, 8, 16)):
        for tt_ in range(16):
            invc[:, g, tt_] = 1.0 / min(tt_ + 1, w)
    return ident, mask4, rmask, invc.reshape(128, 64)


def build(n_pass, TP, NTMAX=512):
    TS = 64
    T = TP + TS
    NG = TP // 128
    nc = bass.Bass("TRN2", target_bir_lowering=False)
    es = ExitStack()

    def din(name, shape):
        return nc.dram_tensor(name, list(shape), F32, kind="ExternalInput").ap()

    def dout(name, shape):
        return nc.dram_tensor(name, list(shape), F32, kind="ExternalOutput").ap()

    xp_d = din("xp", [n_pass * TP, D])
    xs_d = din("xs", [n_pass, TS, D])
    sconf_d = din("s_conf", [n_pass, 30, D])
    spool_d = din("s_pool", [n_pass, 15, D])
    ssc_d = din("s_sconv", [n_pass, 2, D])
    shg_d = din("s_hgrn", [n_pass, 8, 128, 128])
    sffn_d = din("s_ffn", [4, n_pass, 2, DFF])
    vec_d = din("vec", [128, NVEC])
    ident_d = din("c_ident", [128, 128])
    mask_d = din("c_mask4", [128, 128])
    rmask_d = din("c_rmask", [128, 512])
    invc_d = din("c_invc", [128, 64])
    w1_d = din("conf_w_pw1", [D, 2 * D])
    w2_d = din("conf_w_pw2", [D, D])
    pw_d = din("pool_w", [4, 256, 256])
    swin_d = din("sconv_w_in", [D, 3 * D])
    swout_d = din("sconv_w_out", [D, D])
    hq_d = din("hgrn_w_q", [D, D])
    hf_d = din("hgrn_w_f", [D, D])
    hi_d = din("hgrn_w_i", [D, D])
    hg_d = din("hgrn_w_g", [D, D])
    ho_d = din("hgrn_w_o", [D, D])
    fg_d = din("ffn_w_gate", [4, D, DFF])
    fu_d = din("ffn_w_up", [4, D, DFF])
    fd_d = din("ffn_w_down", [4, DFF, D])

    yp_d = dout("yp", [n_pass * TP, D])
    ys_d = dout("ys", [n_pass, TS, D])
    oconf_p = dout("o_conf_p", [30, D])
    oconf_s = dout("o_conf_s", [n_pass, 30, D])
    opool_p = dout("o_pool_p", [15, D])
    opool_s = dout("o_pool_s", [n_pass, 15, D])
    osc_p = dout("o_sconv_p", [2, D])
    osc_s = dout("o_sconv_s", [n_pass, 2, D])
    ohg_p = dout("o_hgrn_p", [8, 128, 128])
    ohg_s = dout("o_hgrn_s", [n_pass, 8, 128, 128])
    offn_p = dout("o_ffn_p", [4, 2, DFF])
    offn_s = dout("o_ffn_s", [4, n_pass, 2, DFF])

    with es:
        P = Prog(nc)

        def sb(name, shape, dt):
            return TT(name, es.enter_context(nc.sbuf_tensor("sb_" + name, list(shape), dt)))

        x = sb("x", [128, NC8, T], F32)
        xn = sb("xn", [128, NC8, T], BF16)
        NSCR = 28800
        scr = sb("scr", [128, NSCR], BF16)
        NSLOT = 3
        SLOT = 5632
        wring = sb("wring", [128, NSLOT, SLOT], BF16)
        vec = sb("vec", [128, NVEC], F32)
        ident = sb("ident", [128, 128], F32)
        identb = sb("identb", [128, 128], BF16)
        onesb = sb("onesb", [128, 128], BF16)
        mask4 = sb("mask4", [128, 128], F32)
        rmask = sb("rmask", [128, 512], F32)
        invc = sb("invc", [128, 64], F32)
        stg = sb("stg", [128, 2, 1024], F32)
        rstd = sb("rstd", [128, T], F32)
        tmpa = sb("tmpa", [128, 2, 512], F32)
        tmpb = sb("tmpb", [128, 2, 512], F32)
        sqb = sb("sqb", [128, 6, 512], BF16)
        gbuf = sb("gbuf", [128, 2, 516], F32)
        accb = sb("accb", [128, 2, 512], F32)
        silb = sb("silb", [128, 2, 512], F32)
        confh_p = sb("confh_p", [128, NC8, 30], BF16)
        confh_s = sb("confh_s", [128, NC8, 30], BF16)
        utail = sb("utail", [128, 2, NC8, 32], F32)
        poolh_p = sb("poolh_p", [128, NC8, 15], F32)
        poolh_s = sb("poolh_s", [128, NC8, 15], F32)
        sch_p = sb("sch_p", [128, NC8, 2], F32)
        sch_s = sb("sch_s", [128, NC8, 2], F32)
        ffnh_p = sb("ffnh_p", [128, 4, NF, 2], F32)
        ffnh_s = sb("ffnh_s", [128, 4, NF, 2], F32)
        ffnt_s = sb("ffnt_s", [128, NF, 2], F32)
        S32 = sb("S32", [128, 2, 8, 128], F32)
        Sbf = sb("Sbf", [128, 2, 8, 128], BF16)
        lbv = sb("lbv", [128, 8], F32)
        omlv = sb("omlv", [128, 8], F32)
        nomlv = sb("nomlv", [128, 8], F32)
        homlv = sb("homlv", [128, 8], F32)
        lbhv = sb("lbhv", [128, 8], F32)
        lbt = sb("lbt", [128, 8, 4], F32)
        lbs = sb("lbs", [128, 8], F32)
        eblast = sb("eblast", [128, 2, 16], F32)
        Setmp = sb("Setmp", [128, 2, 128], F32)
        pst = TT("psum", es.enter_context(nc.psum_tensor("psum", [128, 8, 512], F32)), gran=2048)

        bank_ctr = [0]

        reserved = set()

        def bank():
            while True:
                b = bank_ctr[0] % 8
                bank_ctr[0] += 1
                if b not in reserved:
                    return b

        rot = {}

        def nxt(name, n=2):
            rot[name] = (rot.get(name, -1) + 1) % n
            return rot[name]

        def apof(v):
            return v.ap if isinstance(v, View) else v

        def rd(*vs):
            return [v for v in vs if isinstance(v, View)]

        def mm(out, lhsT, rhs, start, stop):
            P.add("pe", lambda e: e.matmul(out.ap, lhsT=lhsT.ap, rhs=rhs.ap, start=start, stop=stop),
                  reads=[lhsT, rhs], writes=[out])

        def tr(out, in_, idv):
            P.add("pe", lambda e: e.transpose(out.ap, in_.ap, idv.ap), reads=[in_, idv], writes=[out])

        def act(out, in_, func, bias=None, scale=None, eng="act"):
            kw = {}
            if bias is not None:
                kw["bias"] = apof(bias)
            if scale is not None:
                kw["scale"] = apof(scale)
            P.add(eng, lambda e: e.activation(out=out.ap, in_=in_.ap, func=func, **kw),
                  reads=[in_] + rd(bias, scale), writes=[out])

        def tt(out, a, b, op, eng="dve"):
            P.add(eng, lambda e: e.tensor_tensor(out=out.ap, in0=a.ap, in1=b.ap, op=op),
                  reads=[a, b], writes=[out])

        def ts(out, a, s1, s2, op0, op1=None, eng="dve"):
            if op1 is None:
                P.add(eng, lambda e: e.tensor_scalar(out=out.ap, in0=a.ap, scalar1=apof(s1), scalar2=None, op0=op0),
                      reads=[a] + rd(s1), writes=[out])
            else:
                P.add(eng, lambda e: e.tensor_scalar(out=out.ap, in0=a.ap, scalar1=apof(s1), scalar2=apof(s2),
                                                     op0=op0, op1=op1),
                      reads=[a] + rd(s1, s2), writes=[out])

        def stt(out, a, s, b, op0, op1):
            P.add("dve", lambda e: e.scalar_tensor_tensor(out=out.ap, in0=a.ap, scalar=apof(s), in1=b.ap,
                                                          op0=op0, op1=op1),
                  reads=[a, b] + rd(s), writes=[out])

        def cp(out, in_, eng="act"):
            if eng == "act":
                P.add("act", lambda e: e.activation(out=out.ap, in_=in_.ap, func=AF.Copy), reads=[in_], writes=[out])
            else:
                P.add(eng, lambda e: e.tensor_copy(out=out.ap, in_=in_.ap), reads=[in_], writes=[out])

        def memset(v, val, eng="dve"):
            P.add(eng, lambda e: e.memset(v.ap, val), writes=[v])

        def dma_in(dst, src_ap, key, eng="sp"):
            P.add(eng, lambda e: e.dma_start(out=dst.ap, in_=src_ap), writes=[dst], dma_key=key)

        def dma_out(dst_ap, src, key, eng="sp"):
            P.add(eng, lambda e: e.dma_start(out=dst_ap, in_=src.ap), reads=[src], dma_key=key)

        def V(name, c0=0, n=1):
            o = VOFF[name] + c0
            return vec[:, o:o + n]

        blocks = []

        def wblock(parts):
            blocks.append(parts)
            return len(blocks) - 1

        wstate = {"issued": 0}

        def issue_block(i):
            s = i % NSLOT
            off = 0
            for (w2d, col0, ncols, KC) in blocks[i]:
                src = w2d.rearrange("(kc p) n -> p kc n", p=128)[:, :, col0:col0 + ncols]
                dstv = View(wring.res, wring.h[:, s, off:off + KC * ncols].rearrange("p (k n) -> p k n", k=KC))
                P.add("pool", lambda e, dstv=dstv, src=src: e.dma_start(out=dstv.ap, in_=src),
                      writes=[dstv], dma_key="w%d" % s)
                off += KC * ncols
            assert off <= SLOT

        class WB:
            def __init__(self, i):
                self.i = i
                self.s = i % NSLOT
                self.offs = []
                off = 0
                for (w2d, col0, ncols, KC) in blocks[i]:
                    self.offs.append((off, ncols, KC))
                    off += KC * ncols

            def lhsT(self, part, kc, c0, n=128):
                off, ncols, KC = self.offs[part]
                b = off + kc * ncols + c0
                return wring[:, self.s, b:b + n]

        def wget(i):
            assert i >= wstate.get("last", 0), (i, wstate)
            wstate["last"] = i
            while wstate["issued"] < min(len(blocks), i + NSLOT):
                issue_block(wstate["issued"])
                wstate["issued"] += 1
            return WB(i)

        plan = []
        for p in range(n_pass):
            d = {}
            d["w1"] = [wblock([(w1_d, 256 * j, 256, 8), (w1_d, 1024 + 256 * j, 256, 8)]) for j in range(4)]
            d["w2"] = [wblock([(w2_d, 512 * j, 512, 8)]) for j in range(2)]
            d["ffn"] = []
            for L in range(4):
                if L == 1:
                    d["pool"] = wblock([(pw_d[g], 0, 256, 2) for g in range(4)])
                if L == 2:
                    d["swin"] = [wblock([(swin_d, 128 * j, 128, 8), (swin_d, 1024 + 128 * j, 128, 8),
                                         (swin_d, 2048 + 128 * j, 128, 8)]) for j in range(8)]
                    d["swout"] = [wblock([(swout_d, 512 * j, 512, 8)]) for j in range(2)]
                if L == 3:
                    d["hqf"] = []
                    d["hg"] = []
                    for j in range(4):
                        d["hqf"].append(wblock([(hq_d, 256 * j, 256, 8), (hf_d, 256 * j, 256, 8)]))
                        d["hg"].append(wblock([(hi_d, 256 * j, 256, 8), (hg_d, 256 * j, 256, 8)]))
                    d["ho"] = [wblock([(ho_d, 512 * j, 512, 8)]) for j in range(2)]
                gu = [wblock([(fg_d[L], 256 * j, 256, 8), (fu_d[L], 256 * j, 256, 8)]) for j in range(11)]
                dn = [wblock([(fd_d[L], 256 * j, 256, NF)]) for j in range(4)]
                d["ffn"].append((gu, dn))
            plan.append(d)

        ntiles = []
        n0 = 0
        while n0 < TP:
            n1 = min(TP, n0 + NTMAX)
            ntiles.append((n0, n1))
            n0 = n1
        ntiles.append((TP, T))
        groups = [(g * 128, 128, 0) for g in range(NG)] + [(TP, TS, 1)]

        def ecol(n, H):
            return n + H if n < TP else n + 2 * H

        dma_in(vec[:, :], vec_d, "c0")
        dma_in(ident[:, :], ident_d, "c0")
        dma_in(mask4[:, :], mask_d, "c0")
        dma_in(rmask[:, :], rmask_d, "c0")
        dma_in(invc[:, :], invc_d, "c0")
        cp(identb[:, :], ident[:, :])
        memset(onesb[:, :], 1.0)
        memset(confh_p[:, :, :], 0.0)
        memset(poolh_p[:, :, :], 0.0)
        memset(sch_p[:, :, :], 0.0)
        memset(ffnh_p[:, :, :, :], 0.0)
        memset(S32[:, 0, :, :], 0.0)
        memset(Sbf[:, 0, :, :], 0.0)
        lg = View(vec.res, vec.h[:, VOFF["lb_logits"]:VOFF["lb_logits"] + 32].rearrange("p (l c) -> p c l", l=4))
        P.add("dve", lambda e: e.tensor_reduce(out=lbs.h[:, :], in_=lg.ap, op=ALU.max, axis=mybir.AxisListType.X),
              reads=[lg], writes=[lbs[:, :]])
        for l in range(4):
            tt(lbt[:, :, l], vec[:, VOFF["lb_logits"] + 8 * l:VOFF["lb_logits"] + 8 * l + 8], lbs[:, :], ALU.subtract)
        act(lbt[:, :, :], lbt[:, :, :], AF.Exp)
        P.add("dve", lambda e: e.tensor_reduce(out=lbs.h[:, :], in_=lbt.h[:, :, :], op=ALU.add, axis=mybir.AxisListType.X),
              reads=[lbt[:, :, :]], writes=[lbs[:, :]])
        P.add("dve", lambda e: e.reciprocal(out=lbs.h[:, :], in_=lbs.h[:, :]), reads=[lbs[:, :]], writes=[lbs[:, :]])
        tt(omlv[:, :], lbt[:, :, 0], lbs[:, :], ALU.mult)
        ts(lbv[:, :], omlv[:, :], -1.0, 1.0, ALU.mult, ALU.add)
        ts(nomlv[:, :], omlv[:, :], -0.5, None, ALU.mult)
        ts(homlv[:, :], omlv[:, :], 0.5, None, ALU.mult)
        tt(lbhv[:, :], lbv[:, :], homlv[:, :], ALU.add)
        for t_ in (vec, ident, identb, onesb, mask4, rmask, invc):
            pass

        def pbank(b, n, c0=0):
            return pst[:, b, c0:c0 + n]

        def load_fm(dram2d, R, W, dst_fn):
            w0 = 0
            while w0 < W:
                wl = min(1024, W - w0)
                s = nxt("stg")
                dma_in(stg[0:R, s, 0:wl], dram2d[:, w0:w0 + wl], "stg%d" % s)
                nch = wl // 128
                c = 0
                while c < nch:
                    k = min(4, nch - c)
                    b = bank()
                    for q in range(k):
                        tr(pst[:, b, q * 32:q * 32 + R], stg[0:R, s, (c + q) * 128:(c + q + 1) * 128], ident[0:R, 0:R])
                    src = View(pst.res, pst.h[:, b, 0:k * 32].rearrange("p (k r) -> p k r", k=k)[:, :, 0:R], pst.gran)
                    cp(dst_fn(w0 // 128 + c, k), src)
                    c += k
                w0 += wl

        def store_fm(src_fn, R, W, dram2d, key):
            w0 = 0
            while w0 < W:
                wl = min(1024, W - w0)
                s = nxt("stg")
                nch = wl // 128
                c = 0
                while c < nch:
                    k = min(4, nch - c)
                    b = bank()
                    for q in range(k):
                        tr(pst[0:R, b, q * 128:(q + 1) * 128], src_fn(w0 // 128 + c + q), ident[:, :])
                    cp(stg[0:R, s, c * 128:(c + k) * 128], pst[0:R, b, 0:k * 128])
                    c += k
                dma_out(dram2d[:, w0:w0 + wl], stg[0:R, s, 0:wl], key + str(s))
                w0 += wl

        nst = {"banks": None, "prev": [], "cur": []}

        def stats_begin():
            bs = []
            for _ in ntiles:
                b = bank()
                reserved.add(b)
                bs.append(b)
            nst["banks"] = bs
            nst["prev"] = []
            nst["cur"] = []

        def resid_done(m, ti, n0, n1):
            n = n1 - n0
            q = nxt("sqb", 6)
            act(sqb[:, q, 0:n], x[:, m, n0:n1], AF.Square)
            b = nst["banks"][ti]
            nst["cur"].append(lambda: mm(pbank(b, n), onesb[:, :], sqb[:, q, 0:n], m == 0, m == NC8 - 1))

        def stats_step():
            for fn in nst["prev"]:
                fn()
            nst["prev"] = nst["cur"]
            nst["cur"] = []

        def stats_end():
            for fn in nst["prev"] + nst["cur"]:
                fn()
            nst["prev"] = []
            nst["cur"] = []

        def rmsnorm(gname, gc0, out_fn, eng_sq="act"):
            pre = nst["banks"]
            nst["banks"] = None
            for ti, (n0, n1) in enumerate(ntiles):
                n = n1 - n0
                if pre is None:
                    b = bank()
                    for c in range(NC8):
                        q = nxt("sqb", 6)
                        act(sqb[:, q, 0:n], x[:, c, n0:n1], AF.Square)
                        mm(pbank(b, n), onesb[:, :], sqb[:, q, 0:n], c == 0, c == NC8 - 1)
                else:
                    b = pre[ti]
                    reserved.discard(b)
                r = nxt("tmpa")
                act(tmpa[:, r, 0:n], pbank(b, n), AF.Ln, bias=epsv[:, 0:1], scale=1.0 / D)
                act(rstd[:, n0:n1], tmpa[:, r, 0:n], AF.Exp, scale=-0.5)
                for c in range(NC8):
                    stt(out_fn(c, n0, n1), x[:, c, n0:n1], V(gname, gc0 + c), rstd[:, n0:n1], ALU.mult, ALU.mult)

        epsv = sb("epsv", [128, 1], F32)
        memset(epsv[:, :], EPS)

        def proj(wb, part, c0, rhs_fn, KC, n0, n1):
            b = bank()
            n = n1 - n0
            for kc in range(KC):
                mm(pbank(b, n), wb.lhsT(part, kc, c0), rhs_fn(kc), kc == 0, kc == KC - 1)
            return pbank(b, n)

        def xn_rhs(n0, n1):
            return lambda kc: xn[:, kc, n0:n1]

        def carve(off_bf16, shape, dt):
            n = int(np.prod(shape))
            if dt == F32:
                assert off_bf16 % 2 == 0
                ap = scr.h[:, off_bf16:off_bf16 + 2 * n].bitcast(F32)
                nb = 2 * n
            else:
                ap = scr.h[:, off_bf16:off_bf16 + n]
                nb = n
            if len(shape) == 2:
                ap = ap.rearrange("p (a b) -> p a b", a=shape[0])
            elif len(shape) == 3:
                ap = ap.rearrange("p (a b c) -> p a b c", a=shape[0], b=shape[1])
            assert off_bf16 + nb <= NSCR, (off_bf16, nb)
            return scr.sub(ap), off_bf16 + nb

        def ffn(L, p, last):
            gu, dn = plan[p]["ffn"][L]
            h, _ = carve(0, [NF, T], BF16)
            rmsnorm("norm_ffn", 8 * L, lambda c, n0, n1: xn[:, c, n0:n1])
            wv = VOFF["ffn_wdw"] + L * 66
            pending = []
            for f in range(NF):
                wb = wget(gu[f // 2])
                c0 = (f % 2) * 128
                prev = None
                for (n0, n1) in ntiles:
                    n = n1 - n0
                    g = nxt("gbuf")
                    if n0 == 0:
                        cp(gbuf[:, g, 0:2], ffnh_p[:, L, f, :])
                    elif n0 == TP:
                        cp(gbuf[:, g, 0:2], ffnh_s[:, L, f, :])
                    else:
                        cp(gbuf[:, g, 0:2], gbuf[:, prev[0], prev[1]:prev[1] + 2])
                    pg = proj(wb, 0, c0, xn_rhs(n0, n1), 8, n0, n1)
                    pu = proj(wb, 1, c0, xn_rhs(n0, n1), 8, n0, n1)
                    cp(gbuf[:, g, 2:2 + n], pg)
                    a = nxt("accb")
                    act(accb[:, a, 0:n], pg, AF.Copy, scale=vec[:, wv + 44 + f:wv + 44 + f + 1])
                    stt(accb[:, a, 0:n], gbuf[:, g, 0:n], vec[:, wv + f:wv + f + 1],
                        accb[:, a, 0:n], ALU.mult, ALU.add)
                    stt(accb[:, a, 0:n], gbuf[:, g, 1:1 + n], vec[:, wv + 22 + f:wv + 22 + f + 1],
                        accb[:, a, 0:n], ALU.mult, ALU.add)
                    prev = (g, n)
                    if n1 == TP:
                        cp(ffnh_p[:, L, f, :], gbuf[:, g, n:n + 2])
                    if n1 == T:
                        cp(ffnt_s[:, f, :], gbuf[:, g, n:n + 2])
                    for fn in pending:
                        fn()
                    pending.clear()

                    def stage_b(a=a, n=n, f=f, n0=n0, n1=n1, pu=pu):
                        s_ = nxt("silb")
                        act(silb[:, s_, 0:n], accb[:, a, 0:n], AF.Silu, bias=V("ffn_bdw", L * 22 + f))
                        tt(h[:, f, n0:n1], silb[:, s_, 0:n], pu, ALU.mult)
                    pending.append(stage_b)
            for fn in pending:
                fn()
            pending.clear()
            stats_begin()
            for m in range(NC8):
                wb = wget(dn[m // 2])
                c0 = (m % 2) * 128
                for ti, (n0, n1) in enumerate(ntiles):
                    pd = proj(wb, 0, c0, lambda kc, n0=n0, n1=n1: h[:, kc, n0:n1], NF, n0, n1)
                    tt(x[:, m, n0:n1], pd, x[:, m, n0:n1], ALU.add)
                    resid_done(m, ti, n0, n1)
                stats_step()
            stats_end()
            store_fm(lambda c: ffnt_s[:, c, :], 2, DFF, offn_s[L, p], "ost")
            if last:
                store_fm(lambda c: ffnh_p[:, L, c, :], 2, DFF, offn_p[L], "ost")

        ND_TAPS = 6

        def conformer(p, last):
            d = plan[p]
            EXT = T + 60
            cf, o = carve(0, [NC8, T], F32)
            diag, o = carve(o, [2, 31, 128], BF16)
            ub, o = carve(o, [3, EXT], BF16)
            rmsnorm("norm_mix", 0, lambda c, n0, n1: xn[:, c, n0:n1])
            def stage_a(m):
                wb = wget(d["w1"][m // 2])
                c0 = (m % 2) * 128
                u = nxt("ub", 3)
                dg = nxt("diag")
                for j in range(ND_TAPS, 31):
                    ts(diag[:, dg, j, :], identb[:, :], V("conf_wdw", j * 8 + m), None, ALU.mult)
                cp(ub[:, u, 0:30], confh_p[:, m, :])
                cp(ub[:, u, TP + 30:TP + 60], confh_s[:, m, :])
                for (n0, n1) in ntiles:
                    n = n1 - n0
                    pa = proj(wb, 0, c0, xn_rhs(n0, n1), 8, n0, n1)
                    pg = proj(wb, 1, c0, xn_rhs(n0, n1), 8, n0, n1)
                    r = nxt("tmpa")
                    act(tmpa[:, r, 0:n], pg, AF.Sigmoid, bias=V("conf_b1", 8 + m))
                    r2 = nxt("tmpb")
                    stt(tmpb[:, r2, 0:n], pa, V("conf_b1", m), tmpa[:, r, 0:n], ALU.add, ALU.mult)
                    e0 = ecol(n0, 30)
                    cp(ub[:, u, e0:e0 + n], tmpb[:, r2, 0:n])
                    seg = 0 if n0 < TP else 1
                    send = TP if seg == 0 else T
                    if n1 == send:
                        cp(utail[:, seg, m, :], tmpb[:, r2, n - 32:n], eng="pool")
                cp(confh_p[:, m, :], ub[:, u, TP:TP + 30])
                return (u, dg)

            def stage_b(m, u, dg):
                assert len(ntiles) <= 4
                accs = [accb[:, 0, :], accb[:, 1, :], silb[:, 0, :], silb[:, 1, :]]
                for j in range(ND_TAPS):
                    for ti, (n0, n1) in enumerate(ntiles):
                        n = n1 - n0
                        e0 = ecol(n0, 30) - 30
                        acc = View(accs[ti].res, accs[ti].ap[:, 0:n])
                        src = ub[:, u, e0 + j:e0 + j + n]
                        if j == 0:
                            ts(acc, src, V("conf_wdw", j * 8 + m), None, ALU.mult)
                        else:
                            stt(acc, src, V("conf_wdw", j * 8 + m), acc, ALU.mult, ALU.add)
                for ti, (n0, n1) in enumerate(ntiles):
                    n = n1 - n0
                    e0 = ecol(n0, 30) - 30
                    acc = View(accs[ti].res, accs[ti].ap[:, 0:n])
                    b = bank()
                    for j in range(ND_TAPS, 31):
                        mm(pbank(b, n), diag[:, dg, j, :], ub[:, u, e0 + j:e0 + j + n], j == ND_TAPS, j == 30)
                    stt(cf[:, m, n0:n1], pbank(b, n), V("conf_bdw", m), acc, ALU.add, ALU.add)

            st = {}
            st[0] = stage_a(0)
            for m in range(NC8):
                if m + 1 < NC8:
                    st[m + 1] = stage_a(m + 1)
                stage_b(m, *st[m])
            for (n0, n1) in ntiles:
                n = n1 - n0
                b1 = bank()
                b2 = bank()
                for c in range(NC8):
                    q = nxt("sqb", 6)
                    cp(sqb[:, q, 0:n], cf[:, c, n0:n1])
                    mm(pbank(b1, n), onesb[:, :], sqb[:, q, 0:n], c == 0, c == NC8 - 1)
                    q = nxt("sqb", 6)
                    act(sqb[:, q, 0:n], cf[:, c, n0:n1], AF.Square)
                    mm(pbank(b2, n), onesb[:, :], sqb[:, q, 0:n], c == 0, c == NC8 - 1)
                r = nxt("tmpa")
                mean = tmpa[:, r, 0:n]
                ts(mean, pbank(b1, n), 1.0 / D, None, ALU.mult)
                r2 = nxt("tmpb")
                msq = tmpb[:, r2, 0:n]
                tt(msq, mean, mean, ALU.mult)
                r3 = nxt("silb")
                var = silb[:, r3, 0:n]
                stt(var, pbank(b2, n), 1.0 / D, msq, ALU.mult, ALU.subtract)
                act(var, var, AF.Ln, bias=epsv[:, 0:1])
                act(rstd[:, n0:n1], var, AF.Exp, scale=-0.5)
                for c in range(NC8):
                    a = nxt("accb")
                    tt(accb[:, a, 0:n], cf[:, c, n0:n1], mean, ALU.subtract)
                    tt(accb[:, a, 0:n], accb[:, a, 0:n], rstd[:, n0:n1], ALU.mult)
                    act(xn[:, c, n0:n1], accb[:, a, 0:n], AF.Silu, bias=V("conf_lnb", c), scale=V("conf_lng", c))
            stats_begin()
            for m in range(NC8):
                wb = wget(d["w2"][m // 4])
                c0 = (m % 4) * 128
                for ti, (n0, n1) in enumerate(ntiles):
                    pd = proj(wb, 0, c0, xn_rhs(n0, n1), 8, n0, n1)
                    stt(x[:, m, n0:n1], pd, V("conf_b2", m), x[:, m, n0:n1], ALU.add, ALU.add)
                    resid_done(m, ti, n0, n1)
                stats_step()
            stats_end()
            store_fm(lambda c: utail[:, 1, c, 2:32], 30, D, oconf_s[p], "ost")
            if last:
                store_fm(lambda c: utail[:, 0, c, 2:32], 30, D, oconf_p, "ost")

        def poolmix(p, last):
            d = plan[p]
            EXT = T + 30
            xe, o = carve(0, [NC8, EXT], F32)
            tsb, o = carve(o, [4, EXT], F32)
            dfb = xn
            memset(tsb[:, :, 0:16], 0.0)
            for c in range(NC8):
                cp(xe[:, c, 0:15], poolh_p[:, c, :], eng="pool")
                cp(xe[:, c, TP + 15:TP + 30], poolh_s[:, c, :], eng="pool")
            rmsnorm("norm_mix", 8, lambda c, n0, n1: xe[:, c, ecol(n0, 15):ecol(n0, 15) + (n1 - n0)])
            for c in range(NC8):
                cp(poolh_p[:, c, :], xe[:, c, TP:TP + 15], eng="pool")
            wb = wget(d["pool"])
            stats_begin()
            for g in range(4):
                w = (2, 4, 8, 16)[g]
                for c in (2 * g, 2 * g + 1):
                    on_pool = c >= 4
                    sh = 1
                    cur = None
                    while sh < w:
                        a = nxt("tsbP" if on_pool else "tsbD") + (2 if on_pool else 0)
                        srcv = (lambda lo, hi, c=c: xe[:, c, lo:hi]) if cur is None else (lambda lo, hi, cur=cur: tsb[:, cur, lo:hi])
                        tt(tsb[:, a, sh:EXT], srcv(sh, EXT), srcv(0, EXT - sh), ALU.add, eng=("pool" if on_pool else "dve"))
                        cur = a
                        sh *= 2
                    for seg, (t0, n) in enumerate(((0, TP), (TP, TS))):
                        e0 = ecol(t0, 15)
                        stt(dfb[:, c, t0:t0 + n], tsb[:, cur, e0:e0 + n], 1.0 / w, xe[:, c, e0:e0 + n], ALU.mult, ALU.subtract)
                    if p == 0:
                        r = nxt("tmpa")
                        tt(tmpa[:, r, 0:16], tsb[:, cur, 15:31], invc[:, g * 16:(g + 1) * 16], ALU.mult)
                        tt(dfb[:, c, 0:16], tmpa[:, r, 0:16], xe[:, c, 15:31], ALU.subtract)
                for m in (2 * g, 2 * g + 1):
                    for ti, (n0, n1) in enumerate(ntiles):
                        n = n1 - n0
                        b = bank()
                        for kc in range(2):
                            mm(pbank(b, n), wb.lhsT(g, kc, (m % 2) * 128), dfb[:, 2 * g + kc, n0:n1], kc == 0, kc == 1)
                        stt(x[:, m, n0:n1], pbank(b, n), V("pool_scale", m), x[:, m, n0:n1], ALU.mult, ALU.add)
                        resid_done(m, ti, n0, n1)
                    stats_step()
            stats_end()
            store_fm(lambda c: xe[:, c, T + 15:T + 30], 15, D, opool_s[p], "ost")
            if last:
                store_fm(lambda c: poolh_p[:, c, :], 15, D, opool_p, "ost")

        def sconv(p, last):
            d = plan[p]
            EXT = T + 4
            gt, o = carve(0, [NC8, T], BF16)
            pv, o = carve(o, [2, EXT], F32)
            bgb, o = carve(o, [2, T], F32)
            rmsnorm("norm_mix", 16, lambda c, n0, n1: xn[:, c, n0:n1])
            wv = VOFF["sconv_wdw"]
            for m in range(NC8):
                wb = wget(d["swin"][m])
                c0 = 0
                q = nxt("pv")
                cp(pv[:, q, 0:2], sch_p[:, m, :], eng="pool")
                cp(pv[:, q, TP + 2:TP + 4], sch_s[:, m, :], eng="pool")
                for (n0, n1) in ntiles:
                    n = n1 - n0
                    e0 = ecol(n0, 2)
                    pbg = proj(wb, 0, c0, xn_rhs(n0, n1), 8, n0, n1)
                    pcg = proj(wb, 1, c0, xn_rhs(n0, n1), 8, n0, n1)
                    pvv = proj(wb, 2, c0, xn_rhs(n0, n1), 8, n0, n1)
                    r = nxt("tmpa")
                    cp(tmpa[:, r, 0:n], pcg)
                    tt(pv[:, q, e0:e0 + n], tmpa[:, r, 0:n], pvv, ALU.mult)
                    cp(bgb[:, q, n0:n1], pbg)
                    a = nxt("accb")
                    ts(accb[:, a, 0:n], pv[:, q, e0 - 2:e0 - 2 + n], vec[:, wv + m:wv + m + 1], None, ALU.mult)
                    stt(accb[:, a, 0:n], pv[:, q, e0 - 1:e0 - 1 + n], vec[:, wv + 8 + m:wv + 8 + m + 1],
                        accb[:, a, 0:n], ALU.mult, ALU.add)
                    stt(accb[:, a, 0:n], pv[:, q, e0:e0 + n], vec[:, wv + 16 + m:wv + 16 + m + 1],
                        accb[:, a, 0:n], ALU.mult, ALU.add)
                    tt(gt[:, m, n0:n1], accb[:, a, 0:n], bgb[:, q, n0:n1], ALU.mult)
                cp(sch_p[:, m, :], pv[:, q, TP:TP + 2], eng="pool")
                cp(sch_s[:, m, :], pv[:, q, T + 2:T + 4], eng="pool")
            stats_begin()
            for m in range(NC8):
                wb = wget(d["swout"][m // 4])
                c0 = (m % 4) * 128
                for ti, (n0, n1) in enumerate(ntiles):
                    pd = proj(wb, 0, c0, lambda kc, n0=n0, n1=n1: gt[:, kc, n0:n1], 8, n0, n1)
                    tt(x[:, m, n0:n1], pd, x[:, m, n0:n1], ALU.add)
                    resid_done(m, ti, n0, n1)
                stats_step()
            stats_end()
            store_fm(lambda c: sch_s[:, c, :], 2, D, osc_s[p], "ost")
            if last:
                store_fm(lambda c: sch_p[:, c, :], 2, D, osc_p, "ost")

        def hgrn(p, last):
            d = plan[p]
            NGR = len(groups)
            og, o = carve(0, [NC8, T], BF16)
            qd, o = carve(o, [2, T], BF16)
            kd, o = carve(o, [2, T], BF16)
            ovf = o
            vf, o = carve(o, [2, T], BF16)
            vf32 = scr.sub(scr.h[:, ovf:ovf + 2 * T].bitcast(F32))

            def gsl(hb, n0, n1):
                return rstd[:, n0:n1] if hb == 0 else vf32[:, n0:n1]
            kdt, o = carve(o, [2, NGR, 128], BF16)
            vtk, o = carve(o, [2, NGR, 128], BF16)
            scm, o = carve(o, [2, NGR, 128], BF16)
            o32, o = carve(o, [2, T], F32)
            bcu, o = carve(o, [2, 512], F32)
            rmsnorm("norm_mix", 24, lambda c, n0, n1: xn[:, c, n0:n1])
            pstb = pst.sub(pst.h[:, :, :].bitcast(BF16))
            for pr in range(4):
                pair = (2 * pr, 2 * pr + 1)
                wb = wget(d["hqf"][pr])
                for (n0, n1) in ntiles:
                    n = n1 - n0
                    info = {}
                    for h in pair:
                        c0 = (h % 2) * 128
                        pq = proj(wb, 0, c0, xn_rhs(n0, n1), 8, n0, n1)
                        pf = proj(wb, 1, c0, xn_rhs(n0, n1), 8, n0, n1)
                        rq = nxt("silb")
                        act(silb[:, rq, 0:n], pq, AF.Silu)
                        rs = nxt("tmpa")
                        act(tmpa[:, rs, 0:n], pf, AF.Tanh, scale=0.5)
                        info[h] = [rq, rs]
                    for h in pair:
                        rq, rs = info[h]
                        rl = nxt("tmpb")
                        act(tmpb[:, rl, 0:n], tmpa[:, rs, 0:n], AF.Ln, bias=lbhv[:, h:h + 1], scale=homlv[:, h:h + 1])
                        rk = nxt("accb")
                        ts(accb[:, rk, 0:n], tmpa[:, rs, 0:n], nomlv[:, h:h + 1], homlv[:, h:h + 1], ALU.mult, ALU.add)
                        r = nxt("hr")
                        P.add("dve", lambda e, r=r, n=n, rl=rl: e.tensor_tensor_scan(
                            out=bcu.h[:, r, 0:n], data0=rmask.h[:, 0:n], data1=tmpb.h[:, rl, 0:n], initial=0.0,
                            op0=ALU.mult, op1=ALU.add),
                            reads=[rmask[:, 0:n], tmpb[:, rl, 0:n]], writes=[bcu[:, r, 0:n]])
                        info[h] += [rl, rk, r]
                    for h in pair:
                        hb = h % 2
                        rq, rs, rl, rk, r = info[h]
                        act(tmpa[:, rs, 0:n], bcu[:, r, 0:n], AF.Exp)
                        act(tmpb[:, rl, 0:n], bcu[:, r, 0:n], AF.Exp, scale=-1.0)
                        tt(qd[:, hb, n0:n1], silb[:, rq, 0:n], tmpa[:, rs, 0:n], ALU.mult)
                        tt(kd[:, hb, n0:n1], accb[:, rk, 0:n], tmpb[:, rl, 0:n], ALU.mult)
                        for gi, (t0, ntok, seg) in enumerate(groups):
                            if n0 <= t0 < n1:
                                le = t0 + ntok - 1 - n0
                                cp(eblast[:, hb, gi:gi + 1], tmpa[:, rs, le:le + 1], eng="pool")
                wg = wget(d["hg"][pr])
                for h in pair:
                    hb = h % 2
                    for (n0, n1) in ntiles:
                        pvv = proj(wg, 0, (h % 2) * 128, xn_rhs(n0, n1), 8, n0, n1)
                        cp(vf[:, hb, n0:n1], pvv)
                for gi, (t0, ntok, seg) in enumerate(groups):
                    for h in pair:
                        hb = h % 2
                        b = bank()
                        tr(pstb[0:ntok, b, 0:128], kd[:, hb, t0:t0 + ntok], identb[:, :])
                        tr(pstb[0:ntok, b, 128:256], vf[:, hb, t0:t0 + ntok], identb[:, :])
                        cp(kdt[0:ntok, hb, gi, :], pstb[0:ntok, b, 0:128])
                        cp(vtk[0:ntok, hb, gi, :], pstb[0:ntok, b, 128:256], eng="dve")
                for gi, (t0, ntok, seg) in enumerate(groups):
                    for h in pair:
                        hb = h % 2
                        b = bank()
                        mm(pst[0:ntok, b, 0:ntok], kd[:, hb, t0:t0 + ntok], qd[:, hb, t0:t0 + ntok], True, True)
                        tt(scm[0:ntok, hb, gi, 0:ntok], pst[0:ntok, b, 0:ntok], mask4[0:ntok, 0:ntok], ALU.mult)
                gsteps = []
                for h in pair:
                    for (n0, n1) in ntiles:
                        def gst(h=h, n0=n0, n1=n1):
                            pgt = proj(wg, 1, (h % 2) * 128, xn_rhs(n0, n1), 8, n0, n1)
                            act(gsl(h % 2, n0, n1), pgt, AF.Silu)
                        gsteps.append(gst)
                nstep = 0
                for gi, (t0, ntok, seg) in enumerate(groups):
                    for h in pair:
                        hb = h % 2
                        b3 = bank()
                        mm(pst[:, b3, 0:128], kdt[0:ntok, hb, gi, :], vtk[0:ntok, hb, gi, :], True, True)
                        b2 = bank()
                        mm(pst[:, b2, 0:ntok], vtk[0:ntok, hb, gi, :], scm[0:ntok, hb, gi, 0:ntok], True, False)
                        mm(pst[:, b2, 0:ntok], Sbf[:, seg, h, :], qd[:, hb, t0:t0 + ntok], False, True)
                        cp(o32[:, hb, t0:t0 + ntok], pst[:, b2, 0:ntok])
                        ts(Setmp[:, hb, :], S32[:, seg, h, :], eblast[:, hb, gi:gi + 1], None, ALU.mult)
                        stt(Sbf[:, seg, h, :], pst[:, b3, 0:128], eblast[:, hb, gi:gi + 1], Setmp[:, hb, :],
                            ALU.mult, ALU.add)
                        stt(S32[:, seg, h, :], pst[:, b3, 0:128], eblast[:, hb, gi:gi + 1], Setmp[:, hb, :],
                            ALU.mult, ALU.add)
                        nstep += 1
                        if nstep % 3 == 0 and gsteps:
                            gsteps.pop(0)()
                for gst in gsteps:
                    gst()
                for h in pair:
                    hb = h % 2
                    for (n0, n1) in ntiles:
                        n = n1 - n0
                        q = nxt("sqb", 6)
                        tt(sqb[:, q, 0:n], o32[:, hb, n0:n1], o32[:, hb, n0:n1], ALU.mult, eng="pool")
                        b = bank()
                        mm(pbank(b, n), onesb[:, :], sqb[:, q, 0:n], True, True)
                        r = nxt("tmpa")
                        act(tmpa[:, r, 0:n], pbank(b, n), AF.Ln, bias=epsv[:, 0:1], scale=1.0 / 128)
                        act(tmpa[:, r, 0:n], tmpa[:, r, 0:n], AF.Exp, scale=-0.5)
                        a = nxt("accb")
                        stt(accb[:, a, 0:n], o32[:, hb, n0:n1], V("hgrn_ng", 0), tmpa[:, r, 0:n], ALU.mult, ALU.mult)
                        tt(og[:, h, n0:n1], accb[:, a, 0:n], gsl(hb, n0, n1), ALU.mult)
            stats_begin()
            for m in range(NC8):
                wb = wget(d["ho"][m // 4])
                c0 = (m % 4) * 128
                for ti, (n0, n1) in enumerate(ntiles):
                    pd = proj(wb, 0, c0, lambda kc, n0=n0, n1=n1: og[:, kc, n0:n1], 8, n0, n1)
                    tt(x[:, m, n0:n1], pd, x[:, m, n0:n1], ALU.add)
                    resid_done(m, ti, n0, n1)
                stats_step()
            stats_end()
            for h in range(8):
                dma_out(ohg_s[p, h], S32[:, 1, h, :], "ohg")
                if last:
                    dma_out(ohg_p[h], S32[:, 0, h, :], "ohg")

        NL = 4
        lslots = [scr.sub(scr.h[:, 17408 + 2048 * k:17408 + 2048 * (k + 1)].bitcast(F32)) for k in range(NL)]
        assert 17408 + 2048 * NL <= NSCR

        def xload_dma(p, gi):
            t0, ntok, seg = groups[gi]
            src = xp_d[p * TP + t0:p * TP + t0 + 128, :] if seg == 0 else xs_d[p]
            k = gi % NL
            dma_in(lslots[k][0:ntok, :], src, "lst%d" % k)

        def xload_compute(p, gi):
            t0, ntok, seg = groups[gi]
            sl = lslots[gi % NL]
            for half in range(2):
                b = bank()
                for q in range(4):
                    c = half * 4 + q
                    tr(pst[:, b, q * 128:q * 128 + ntok], sl[0:ntok, c * 128:(c + 1) * 128], ident[0:ntok, 0:ntok])
                srcv = View(pst.res, pst.h[:, b, :].rearrange("p (k r) -> p k r", k=4)[:, :, 0:ntok], pst.gran)
                cp(x[:, half * 4:half * 4 + 4, t0:t0 + ntok], srcv, eng=("act" if half == 0 else "dve"))

        for p in range(n_pass):
            last = p == n_pass - 1
            if p == 0:
                for gi in range(min(NL, len(groups))):
                    xload_dma(0, gi)
                for gi in range(len(groups)):
                    xload_compute(0, gi)
                    if gi + NL < len(groups):
                        xload_dma(0, gi + NL)
            load_fm(sconf_d[p], 30, D, lambda c, k: confh_s[:, c:c + k, :])
            load_fm(spool_d[p], 15, D, lambda c, k: poolh_s[:, c:c + k, :])
            load_fm(ssc_d[p], 2, D, lambda c, k: sch_s[:, c:c + k, :])
            for L in range(4):
                load_fm(sffn_d[L, p], 2, DFF, lambda c, k, L=L: ffnh_s[:, L, c:c + k, :])
            for h in range(8):
                dma_in(S32[:, 1, h, :], shg_d[p, h], "shg")
            cp(Sbf[:, 1, :, :], S32[:, 1, :, :], eng="pool")
            conformer(p, last)
            ffn(0, p, last)
            poolmix(p, last)
            ffn(1, p, last)
            sconv(p, last)
            ffn(2, p, last)
            hgrn(p, last)
            ffn(3, p, last)
            xo, _ = carve(0, [NC8, T], F32)
            rmsnorm("norm_final", 0, lambda c, n0, n1: xo[:, c, n0:n1])
            if not last:
                for gi in range(min(NL, len(groups))):
                    xload_dma(p + 1, gi)
            for gi, (t0, ntok, seg) in enumerate(groups):
                s = nxt("stg")
                for half in range(2):
                    b = bank()
                    for q in range(4):
                        c = half * 4 + q
                        tr(pst[0:ntok, b, q * 128:(q + 1) * 128], xo[:, c, t0:t0 + ntok], ident[:, :])
                    cp(stg[0:ntok, s, half * 512:(half + 1) * 512], pst[0:ntok, b, :], eng=("act" if half == 0 else "dve"))
                dst = yp_d[p * TP + t0:p * TP + t0 + 128, :] if seg == 0 else ys_d[p]
                dma_out(dst, stg[0:ntok, s, :], "oy%d" % s)
                if not last:
                    xload_compute(p + 1, gi)
                    if gi + NL < len(groups):
                        xload_dma(p + 1, gi + NL)
        P.emit(es)
    return nc


N_CORES = 8
N_PASS = 4
TP_FULL = 1024
_WNAMES = ["conf_w_pw1", "conf_w_pw2", "pool_w", "sconv_w_in", "sconv_w_out", "hgrn_w_q", "hgrn_w_f",
           "hgrn_w_i", "hgrn_w_g", "hgrn_w_o", "ffn_w_gate", "ffn_w_up", "ffn_w_down"]


def make_in_maps(inp, n_cores, n_pass, TP):
    T = TP + 64
    vec = build_vec(inp)
    ident, mask4, rmask, invc = build_consts(TP, T)
    shared = {"vec": vec, "c_ident": ident, "c_mask4": mask4, "c_rmask": rmask, "c_invc": invc}
    for n in _WNAMES:
        a = np.asarray(inp[n], np.float32)
        shared[n] = np.ascontiguousarray(a[0] if n in ("conf_w_pw1", "conf_w_pw2", "pool_w", "sconv_w_in", "sconv_w_out",
                                                        "hgrn_w_q", "hgrn_w_f", "hgrn_w_i", "hgrn_w_g", "hgrn_w_o") else a)
    maps = []
    for c in range(n_cores):
        sl = slice(c * n_pass, (c + 1) * n_pass)
        m = dict(shared)
        m["xp"] = np.ascontiguousarray(inp["x_prompt"][c], np.float32)
        m["xs"] = np.ascontiguousarray(inp["x_sample"][sl], np.float32)
        m["s_conf"] = np.ascontiguousarray(inp["state_conformer_conv"][0, sl], np.float32)
        m["s_pool"] = np.ascontiguousarray(inp["state_pool"][0, sl], np.float32)
        m["s_sconv"] = np.ascontiguousarray(inp["state_short_conv"][0, sl], np.float32)
        m["s_hgrn"] = np.ascontiguousarray(inp["state_hgrn"][0, sl], np.float32)
        m["s_ffn"] = np.ascontiguousarray(inp["state_ffn_conv"][:, sl], np.float32)
        maps.append(m)
    return maps


def gather(results, n_cores, n_pass):
    def cat(k, axis=0):
        return np.concatenate([np.asarray(r[k], np.float32) for r in results], axis=axis)

    def stk(k):
        return np.stack([np.asarray(r[k], np.float32) for r in results], axis=0)
    y_p = stk("yp")
    y_s = cat("ys")
    return (y_p, y_s,
            stk("o_conf_p")[None], cat("o_conf_s")[None],
            stk("o_pool_p")[None], cat("o_pool_s")[None],
            stk("o_sconv_p")[None], cat("o_sconv_s")[None],
            stk("o_hgrn_p")[None], cat("o_hgrn_s")[None],
            np.stack([np.asarray(r["o_ffn_p"], np.float32) for r in results], axis=1),
            np.concatenate([np.asarray(r["o_ffn_s"], np.float32) for r in results], axis=1))


def kernel(**inputs):
    inp = {k: np.asarray(v) for k, v in inputs.items()}
    nc = build(N_PASS, TP_FULL)
    maps = make_in_maps(inp, N_CORES, N_PASS, TP_FULL)
    res = run_bass_kernel_spmd(nc, maps, core_ids=list(range(N_CORES)))
    return gather(res.results, N_CORES, N_PASS)
```
